# Optimizing a Trainium2 kernel written in Bass

```python
import math
import jax, jax.numpy as jnp
from jax import lax
import numpy as np

D_MODEL = 1024
BATCH = 8
SEQ = 8192
DEPTH = 1

CTX_LEN = 256
GRID_W = 64
W_R = 1280
H_R = 5
BW = W_R // H_R
LRU_C = 8.0
CONV_R = 4
CONV_R_LEFT = 2
W_G = 1024
H_G = 8
GC = W_G // H_G
CHUNK = 128
D_FF = 2816
N_MOD = 6
EPS = 1e-6
OFF_RX = 0
OFF_RG = OFF_RX + W_R
OFF_U = OFF_RG + W_R
OFF_V = OFF_U + W_G
OFF_GR = OFF_V + W_G
OFF_GG = OFF_GR + D_MODEL
N_IN = OFF_GG + D_MODEL

kernel_name = "hybrid_rglru_gmlp_convffn_diffusion_block"


def _rmsnorm(x, g):
    xf = x.astype(jnp.float32)
    y = xf * lax.rsqrt(jnp.mean(xf * xf, axis=-1, keepdims=True) + EPS)
    return (y * g.astype(jnp.float32)).astype(x.dtype)


def _modulate(x, g, shift, scale):
    return _rmsnorm(x, g) * (1.0 + scale) + shift


def _dwconv1d(x, w, b):
    C = x.shape[-1]
    y = lax.conv_general_dilated(x, w[:, None, :], window_strides=(1,),
                                 padding=[(CONV_R_LEFT, CONV_R - 1 - CONV_R_LEFT)],
                                 dimension_numbers=('NWC', 'WIO', 'NWC'),
                                 feature_group_count=C)
    return y + b


def _dwconv2d(x, w, b):
    C = x.shape[-1]
    y = lax.conv_general_dilated(x, w[:, :, None, :], window_strides=(1, 1),
                                 padding=[(1, 1), (1, 1)],
                                 dimension_numbers=('NHWC', 'HWIO', 'NHWC'),
                                 feature_group_count=C)
    return y + b


def _linear_scan(a, b, h0, reverse):
    def step(h, ab):
        a_t, b_t = ab
        h = a_t * h + b_t
        return h, h
    h_last, hs = lax.scan(step, h0, (jnp.swapaxes(a, 0, 1), jnp.swapaxes(b, 0, 1)), reverse=reverse)
    return jnp.swapaxes(hs, 0, 1), h_last


def _rglru_dir(xr, lam, wa, ba, wx, bx, h0, reverse):
    B, T, _ = xr.shape
    xh = xr.reshape(B, T, H_R, BW)
    r = jax.nn.sigmoid(jnp.einsum('bthi,hij->bthj', xh, wa).reshape(B, T, W_R) + ba)
    i = jax.nn.sigmoid(jnp.einsum('bthi,hij->bthj', xh, wx).reshape(B, T, W_R) + bx)
    log_a = (-LRU_C * r * jax.nn.softplus(-lam)).astype(jnp.float32)
    a = jnp.exp(log_a)
    mult = jnp.sqrt(-jnp.expm1(2.0 * log_a))
    bt = mult * (i * xr).astype(jnp.float32)
    hs, h_last = _linear_scan(a, bt, h0, reverse)
    return hs.astype(xr.dtype), h_last


def _rglru_bidir(xr, lam, wa, ba, wx, bx, h0_f, h0_b):
    y_f, h_f = _rglru_dir(xr, lam[0], wa[0], ba[0], wx[0], bx[0], h0_f, False)
    y_b, h_b = _rglru_dir(xr, lam[1], wa[1], ba[1], wx[1], bx[1], h0_b, True)
    return y_f + y_b, h_f, h_b


def _chunk_mlp(u_raw, v_raw, g_v, w_s, b_s):
    B, T, _ = u_raw.shape
    u = jax.nn.gelu(u_raw)
    v = _rmsnorm(jax.nn.gelu(v_raw), g_v).reshape(B, T // CHUNK, CHUNK, H_G, GC)
    s = jnp.einsum('hpq,bnqhc->bnphc', w_s, v) + b_s[None, None, :, :, None]
    return u * s.reshape(B, T, W_G)


def _mixer_out(p, y_lru, g_v, w_s, b_s, w_pr, w_pg, w_out):
    y_r = jax.nn.gelu(p[..., OFF_RG:OFF_RG + W_R]) * y_lru
    y_g = _chunk_mlp(p[..., OFF_U:OFF_U + W_G], p[..., OFF_V:OFF_V + W_G], g_v, w_s, b_s)
    gate_r = jax.nn.sigmoid(p[..., OFF_GR:OFF_GR + D_MODEL])
    gate_g = jax.nn.sigmoid(p[..., OFF_GG:OFF_GG + D_MODEL])
    merged = gate_r * (y_r @ w_pr) + gate_g * (y_g @ w_pg)
    return merged @ w_out


def _conv_ffn(h, w_up, cw, cb, w_down, grid_h, grid_w):
    B, T, _ = h.shape
    up = h @ w_up
    g = _dwconv2d(up[..., :D_FF].reshape(B, grid_h, grid_w, D_FF), cw, cb).reshape(B, T, D_FF)
    return (jax.nn.gelu(g) * up[..., D_FF:]) @ w_down


def setup_inputs(seed: int = 0) -> dict:
    key = jax.random.key(seed)
    ks = jax.random.split(key, 32)
    f32 = jnp.float32

    def nrm(k, shape, fan_in, gain=1.0):
        return (gain * fan_in ** -0.5) * jax.random.normal(k, shape, f32)

    u = jax.random.uniform(ks[8], (DEPTH, 2, W_R), f32, minval=0.9, maxval=0.999)
    a1 = u ** (1.0 / LRU_C)
    lru_lam = jnp.log(a1) - jnp.log1p(-a1)
    return {
        "x": jax.random.normal(ks[0], (BATCH, SEQ, D_MODEL), f32),
        "c": jax.random.normal(ks[1], (BATCH, D_MODEL), f32),
        "ctx": jax.random.normal(ks[2], (BATCH, CTX_LEN, D_MODEL), f32),
        "c_ctx": jax.random.normal(ks[3], (D_MODEL,), f32),
        "w_mod": nrm(ks[4], (DEPTH, D_MODEL, N_MOD * D_MODEL), D_MODEL, 0.5),
        "b_mod": 0.02 * jax.random.normal(ks[5], (DEPTH, N_MOD * D_MODEL), f32),
        "g_norm1": 1.0 + 0.02 * jax.random.normal(ks[6], (DEPTH, D_MODEL), f32),
        "w_in": nrm(ks[7], (DEPTH, D_MODEL, N_IN), D_MODEL),
        "conv_w": nrm(ks[9], (DEPTH, CONV_R, W_R), CONV_R),
        "conv_b": 0.02 * jax.random.normal(ks[10], (DEPTH, W_R), f32),
        "lru_lam": lru_lam,
        "lru_wa": nrm(ks[11], (DEPTH, 2, H_R, BW, BW), BW),
        "lru_ba": 0.02 * jax.random.normal(ks[12], (DEPTH, 2, W_R), f32),
        "lru_wx": nrm(ks[13], (DEPTH, 2, H_R, BW, BW), BW),
        "lru_bx": 0.02 * jax.random.normal(ks[14], (DEPTH, 2, W_R), f32),
        "g_v": 1.0 + 0.02 * jax.random.normal(ks[15], (DEPTH, W_G), f32),
        "w_s": nrm(ks[16], (DEPTH, H_G, CHUNK, CHUNK), CHUNK),
        "b_s": 1.0 + 0.02 * jax.random.normal(ks[17], (DEPTH, CHUNK, H_G), f32),
        "w_pr": nrm(ks[18], (DEPTH, W_R, D_MODEL), W_R),
        "w_pg": nrm(ks[19], (DEPTH, W_G, D_MODEL), W_G),
        "w_out": nrm(ks[20], (DEPTH, D_MODEL, D_MODEL), D_MODEL),
        "g_norm2": 1.0 + 0.02 * jax.random.normal(ks[21], (DEPTH, D_MODEL), f32),
        "w_up": nrm(ks[22], (DEPTH, D_MODEL, 2 * D_FF), D_MODEL),
        "ffn_conv_w": nrm(ks[23], (DEPTH, 3, 3, D_FF), 9.0),
        "ffn_conv_b": 0.02 * jax.random.normal(ks[24], (DEPTH, D_FF), f32),
        "w_down": nrm(ks[25], (DEPTH, D_FF, D_MODEL), D_FF),
        "g_final": 1.0 + 0.02 * jax.random.normal(ks[26], (D_MODEL,), f32),
    }


def reference(x, c, ctx, c_ctx, w_mod, b_mod, g_norm1, w_in, conv_w, conv_b, lru_lam,
              lru_wa, lru_ba, lru_wx, lru_bx, g_v, w_s, b_s, w_pr, w_pg, w_out,
              g_norm2, w_up, ffn_conv_w, ffn_conv_b, w_down, g_final):
    B, T, _ = x.shape
    rows = T // GRID_W
    t_ctx = ctx.shape[1]
    for l in range(DEPTH):
        last = l == DEPTH - 1
        m_x = (jax.nn.silu(c) @ w_mod[l] + b_mod[l]).reshape(B, N_MOD, 1, D_MODEL)
        m_c = (jax.nn.silu(c_ctx) @ w_mod[l] + b_mod[l]).reshape(N_MOD, D_MODEL)
        lru = (lru_lam[l], lru_wa[l], lru_ba[l], lru_wx[l], lru_bx[l])
        gmlp_and_merge = (g_v[l], w_s[l], b_s[l], w_pr[l], w_pg[l], w_out[l])

        hc = _modulate(ctx, g_norm1[l], m_c[0], m_c[1])
        pc = hc @ w_in[l][:, :(W_R if last else N_IN)]
        xr_c = _dwconv1d(pc[..., OFF_RX:OFF_RX + W_R], conv_w[l], conv_b[l])
        h0 = jnp.zeros((B, W_R), jnp.float32)
        yc_lru, hc_f, hc_b = _rglru_bidir(xr_c, *lru, h0, h0)

        hx = _modulate(x, g_norm1[l], m_x[:, 0], m_x[:, 1])
        px = hx @ w_in[l]
        xr_x = _dwconv1d(px[..., OFF_RX:OFF_RX + W_R], conv_w[l], conv_b[l])
        yx_lru, _, _ = _rglru_bidir(xr_x, *lru, hc_f, hc_b)
        x = x + m_x[:, 2] * _mixer_out(px, yx_lru, *gmlp_and_merge)
        if not last:
            ctx_mid = ctx + m_c[2] * _mixer_out(pc, yc_lru, *gmlp_and_merge)

        h2 = _modulate(x, g_norm2[l], m_x[:, 3], m_x[:, 4])
        x = x + m_x[:, 5] * _conv_ffn(h2, w_up[l], ffn_conv_w[l], ffn_conv_b[l], w_down[l], rows, GRID_W)
        if not last:
            h2c = _modulate(ctx_mid, g_norm2[l], m_c[3], m_c[4])
            ctx = ctx_mid + m_c[5] * _conv_ffn(h2c, w_up[l], ffn_conv_w[l], ffn_conv_b[l], w_down[l], 1, t_ctx)
    return _rmsnorm(x, g_final)
```

```python
import numpy as np
from contextlib import ExitStack
import concourse.bass as bass
import concourse.mybir as mybir
from concourse.bass_utils import run_bass_kernel_spmd
from concourse.alu_op_type import AluOpType as ALU

F32 = mybir.dt.float32
BF16 = mybir.dt.bfloat16
I32 = mybir.dt.int32
AF = mybir.ActivationFunctionType

D = 1024
T = 8192
CH = 512
NCH = T // CH
WR = 1280
NIN = 6656
DFF = 2816
NJ = DFF // 128
TCTX = 256
EPS = 1e-6
ARENA_WORDS = 52000
PADR = 8
PADH = 64


class Trk:
    def __init__(self, nc, es):
        self.nc = nc
        self.es = es
        self.eng = {'pe': nc.tensor, 'act': nc.scalar, 'dve': nc.vector, 'pool': nc.gpsimd, 'sp': nc.sync}
        self.semh = {}
        self.cnt = {}
        for e in ['pe', 'act', 'dve', 'pool']:
            self.semh[e] = es.enter_context(nc.semaphore("sem_" + e))
            self.cnt[e] = 0
        self.known = {e: {} for e in self.eng}
        self.lastw = {}
        self.readers = {}
        self.ndsem = 0
        self.dkey2sem = {}

    def _wait(self, e, ev):
        name, val, src = ev
        if self.known[e].get(name, 0) >= val:
            return
        self.eng[e].wait_ge(self.semh[name], val)
        self.known[e][name] = val

    def _deps(self, e, reads, writes):
        for k in reads:
            ev = self.lastw.get(k)
            if ev is not None:
                self._wait(e, ev)
        for k in writes:
            ev = self.lastw.get(k)
            if ev is not None and ev[2] != e:
                self._wait(e, ev)
            rd = self.readers.get(k)
            if rd:
                for name, (val, src) in rd.items():
                    if src != e:
                        self._wait(e, (name, val, src))

    def _record(self, ev, reads, writes):
        for k in reads:
            rd = self.readers.setdefault(k, {})
            old = rd.get(ev[0])
            if old is None or old[0] < ev[1]:
                rd[ev[0]] = (ev[1], ev[2])
        for k in writes:
            self.lastw[k] = ev
            self.readers[k] = {}

    def op(self, e, fn, reads=(), writes=()):
        xr = [k for k in reads if isinstance(k, tuple) and k[0] == 'ps']
        if xr:
            writes = list(writes) + xr
        self._deps(e, reads, writes)
        ins = fn(self.eng[e])
        self.cnt[e] += 1
        ins.then_inc(self.semh[e], 1)
        ev = (e, self.cnt[e], e)
        self._record(ev, reads, writes)
        return ev

    def dma(self, q, out, in_, reads=(), writes=(), semkey=None, **kw):
        self._deps(q, reads, writes)
        name = self.dkey2sem.get(semkey)
        if name is None:
            name = "dma%d" % self.ndsem
            self.ndsem += 1
            self.semh[name] = self.es.enter_context(self.nc.semaphore(name))
            self.cnt[name] = 0
            self.dkey2sem[semkey] = name
        ins = self.eng[q].dma_start(out=out, in_=in_, **kw)
        self.cnt[name] += 16
        ins.then_inc(self.semh[name], 16)
        ev = (name, self.cnt[name], None)
        self._record(ev, reads, writes)
        return ev

    def barrier(self, engines=('pe', 'act', 'dve', 'pool', 'sp')):
        for e in engines:
            for name, c in self.cnt.items():
                if c > 0 and name != e:
                    self._wait(e, (name, c, None))


class Arena:
    def __init__(self, ap):
        self.ap = ap
        self.off = 0
        self.n = ap.shape[1]

    def f32(self, n):
        o = self.off
        self.off += n
        assert self.off <= self.n, ("arena overflow", self.off, self.n)
        return self.ap[:, o:o + n]

    def bf16(self, n):
        w = (n + 1) // 2
        return self.f32(w).bitcast(BF16)[:, 0:n]

    def mark(self):
        return self.off

    def reset(self, m):
        self.off = m


def v3(ap, b):
    return ap.rearrange("p (a b) -> p a b", b=b)


def build_program(debug=False, stop_after=None, dbg_names=()):
    nc = bass.Bass("TRN2", target_bir_lowering=False)
    es = ExitStack()

    def din(name, shape, dt=F32):
        return nc.dram_tensor(name, list(shape), dt, kind="ExternalInput").ap()

    def dscr(name, shape, dt):
        kind = "ExternalOutput" if (debug and name in dbg_names) else "Internal"
        return nc.dram_tensor(name, list(shape), dt, kind=kind).ap()

    x_d = din("x", [T, D]); c_d = din("c", [8, 128]); ctx_d = din("ctx", [TCTX, D]); cctx_d = din("c_ctx", [8, 128])
    wmod_d = din("w_mod", [D, 6 * D]); bmod_d = din("b_mod", [48, 128])
    g1_d = din("g_norm1", [8, 128]); g2_d = din("g_norm2", [8, 128])
    win_d = din("w_in", [D, NIN]); cw_d = din("conv_w", [40, 128]); cb_d = din("conv_b", [10, 128])
    lam_d = din("lru_lam", [20, 128]); ba_d = din("lru_ba", [20, 128]); bx_d = din("lru_bx", [20, 128])
    wa_d = din("lru_wa", [2, 5, 256, 256]); wx_d = din("lru_wx", [2, 5, 256, 256])
    gv_d = din("g_v", [1, D]); ws_d = din("w_s", [8 * 128, 128]); bs_d = din("b_s", [128, 8])
    wpr_d = din("w_pr", [WR, D]); wpg_d = din("w_pg", [D, D]); wout_d = din("w_out", [D, D])
    wup_d = din("w_up", [D, 2 * DFF]); fcw_d = din("ffn_conv_w", [198, 128]); fcb_d = din("ffn_conv_b", [22, 128])
    wdn_d = din("w_down", [DFF, D]); gf_d = din("g_final", [1, D])
    out_d = nc.dram_tensor("out", [T, D], F32, kind="ExternalOutput").ap()

    d_hxT = dscr("d_hxT", [NCH, 128, 8 * CH], BF16)
    d_pxr = dscr("d_pxr", [10, 128, T + 2 * PADR], BF16)
    d_yb = dscr("d_yb", [NCH, 128, 10 * CH], BF16)
    d_yl = dscr("d_yl", [NCH, 128, 10 * CH], BF16)
    d_x1 = dscr("d_x1", [T, D], F32)
    d_h2T = dscr("d_h2T", [8, 128, T + 2 * PADH], BF16)
    d_win = dscr("d_win", [34, 128, 1024], BF16)
    d_wpp = dscr("d_wpp", [8, 128, 18 * 128], BF16)
    d_wup = dscr("d_wup", [44, 128, 1024], BF16)

    arena_t = es.enter_context(nc.sbuf_tensor("arena", [128, ARENA_WORDS], F32))
    A = Arena(arena_t[:, :])
    ps_t = es.enter_context(nc.psum_tensor("ps", [128, 8 * 512], F32))
    PS = [ps_t[:, b * 512:(b + 1) * 512] for b in range(8)]
    PSB = [p.bitcast(BF16) for p in PS]
    tr = Trk(nc, es)

    def psk(b):
        return ('ps', b)

    def finish():
        tr.barrier(('sp',))
        es.close()
        return nc

    dbg_small = nc.dram_tensor('dbg_small', [128, 1024], F32, kind='ExternalOutput').ap() if (debug and stop_after in ('P0', 'LRUc')) else None

    def ACT(out, in_, func, reads, writes, bias=None, scale=None, accum_out=None):
        kw = {}
        if bias is not None:
            kw['bias'] = bias
        if scale is not None:
            kw['scale'] = scale
        if accum_out is not None:
            kw['accum_out'] = accum_out
        return tr.op('act', lambda e: e.activation(out=out, in_=in_, func=func, **kw), reads, writes)

    def TS(eng, out, in0, s1, s2, op0, op1, reads, writes):
        if s2 is None:
            return tr.op(eng, lambda e: e.tensor_scalar(out=out, in0=in0, scalar1=s1, scalar2=None, op0=op0), reads, writes)
        return tr.op(eng, lambda e: e.tensor_scalar(out=out, in0=in0, scalar1=s1, scalar2=s2, op0=op0, op1=op1), reads, writes)

    def TT(eng, out, in0, in1, op, reads, writes):
        return tr.op(eng, lambda e: e.tensor_tensor(out=out, in0=in0, in1=in1, op=op), reads, writes)

    def STT(out, in0, scalar, in1, op0, op1, reads, writes):
        return tr.op('dve', lambda e: e.scalar_tensor_tensor(out=out, in0=in0, scalar=scalar, in1=in1, op0=op0, op1=op1), reads, writes)

    def CP(eng, out, in_, reads, writes):
        return tr.op(eng, lambda e: e.tensor_copy(out=out, in_=in_), reads, writes)

    def MM(ps_ap, pskey, ops):
        n = len(ops)
        for i, (l, r, ks) in enumerate(ops):
            tr.op('pe', lambda e, l=l, r=r, i=i: e.matmul(ps_ap, l, r, start=(i == 0), stop=(i == n - 1)),
                  reads=ks, writes=[pskey])

    def TR(out_ap, pskey, in_ap, ident_ap, reads):
        tr.op('pe', lambda e: e.transpose(out_ap, in_ap, ident_ap), reads=reads, writes=[pskey])

    def LD(out, in_, writes, reads=(), q='sp', **kw):
        return tr.dma(q, out, in_, reads=reads, writes=writes, semkey=writes[0], **kw)

    def STORE(out, in_, reads, writes=(), q='sp', **kw):
        return tr.dma(q, out, in_, reads=reads, writes=writes, semkey=('st', reads[0]), **kw)

    ident = A.f32(128)
    identb = A.bf16(128)
    ones_f = A.f32(128)
    CT = [A.f32(128) for _ in range(4)]
    mods = A.f32(96)
    sc = A.f32(16)
    gs1 = A.f32(8); sh1 = A.f32(8); gs1c = A.f32(8); sh1c = A.f32(8); gs2 = A.f32(8); sh2 = A.f32(8)
    csh = A.f32(20); cs1 = A.f32(20)
    hb = A.f32(40)
    gvb = A.bf16(1024)
    gfb = A.f32(1024)
    wsT = A.bf16(8 * 128)
    bsf = A.f32(1024)
    hstate = A.f32(20)
    tiny = A.f32(64)
    zer = A.bf16(128)
    mh = A.f32(4)
    m25 = A.f32(2048)
    PH0 = A.mark()

    def cbc(kc): return CT[0][:, 56 + kc:57 + kc]
    def cwc(tap, kc): return CT[0][:, 16 + tap * 10 + kc:17 + tap * 10 + kc]
    def hbc(gi, d, kc): return hb[:, gi * 20 + d * 10 + kc:gi * 20 + d * 10 + kc + 1]
    def fcwc(tap, j):
        r = tap * 22 + j
        return CT[1][:, r:r + 1] if r < 128 else CT[2][:, r - 128:r - 127]
    def fcbc(j): return CT[2][:, 70 + j:71 + j]

    iot = A.f32(128)
    tr.op('pool', lambda e: e.iota(iot.bitcast(I32), [[1, 128]], base=0, channel_multiplier=-1), (), ['iot'])
    TS('dve', ident, iot.bitcast(I32), 0, None, ALU.is_equal, None, ['iot'], ['ident'])
    CP('dve', identb, ident, ['ident'], ['identb'])
    tr.op('dve', lambda e: e.memset(ones_f, 1.0), (), ['ones_f'])
    tr.op('dve', lambda e: e.memset(zer, 0.0), (), ['zer'])
    tr.op('dve', lambda e: e.memset(mh, -0.5), (), ['mh'])

    RT = [A.f32(128) for _ in range(4)]
    rows = [
        (0, 0, g1_d, 8), (0, 8, g2_d, 8), (0, 16, cw_d, 40), (0, 56, cb_d, 10), (0, 66, lam_d, 20),
        (0, 86, ba_d, 20), (0, 106, bx_d, 20),
        (1, 0, fcw_d[0:128, :], 128),
        (2, 0, fcw_d[128:198, :], 70), (2, 70, fcb_d, 22), (2, 92, c_d, 8), (2, 100, cctx_d, 8),
        (3, 0, bmod_d, 48),
    ]
    rtk = {i: [] for i in range(4)}
    for (ti, r0, src, n) in rows:
        rtk[ti].append(('RT', ti, r0))
    for i in range(4):
        tr.op('pool', lambda e, i=i: e.memset(RT[i], 0.0), (), rtk[i])
    for (ti, r0, src, n) in rows:
        tr.dma('sp', RT[ti][r0:r0 + n, :], src, reads=(), writes=[('RT', ti, r0)], semkey=('RT', ti, r0))
    for i in range(4):
        TR(PS[i][:, 0:128], psk(i), RT[i], ident, rtk[i] + ['ident'])
        CP('dve', CT[i], PS[i][:, 0:128], [psk(i)], [('CT', i)])

    sc3 = v3(sc, 2)
    ACT(sc3[:, :, 0], CT[2][:, 92:100], AF.Silu, [('CT', 2)], ['sc'])
    ACT(sc3[:, :, 1], CT[2][:, 100:108], AF.Silu, [('CT', 2), 'sc'], ['sc'])

    wm = [A.f32(8 * 1024) for _ in range(2)]
    mods3 = v3(mods, 2)
    for i in range(6):
        s = i % 2
        wm3 = v3(wm[s], 1024)
        LD(wm3, wmod_d[:, i * 1024:(i + 1) * 1024].rearrange("(kc p) n -> p kc n", p=128), [('wm', s)])
        for oc in range(8):
            col = (i * 8 + oc) * 2
            MM(PS[4][:, col:col + 2], psk(4),
               [(wm3[:, kc, oc * 128:(oc + 1) * 128], sc3[:, kc, :], [('wm', s), 'sc']) for kc in range(8)])
    psm3 = v3(PS[4][:, 0:96], 2)
    for n in range(2):
        TT('dve', mods3[:, :, n], psm3[:, :, n], CT[3][:, 0:48], ALU.add, [psk(4), ('CT', 3)], ['mods'])
    STT(gs1, mods3[:, 8:16, 0], 1.0, CT[0][:, 0:8], ALU.add, ALU.mult, ['mods', ('CT', 0)], ['gs1'])
    STT(gs1c, mods3[:, 8:16, 1], 1.0, CT[0][:, 0:8], ALU.add, ALU.mult, ['mods', ('CT', 0)], ['gs1c'])
    STT(gs2, mods3[:, 32:40, 0], 1.0, CT[0][:, 8:16], ALU.add, ALU.mult, ['mods', ('CT', 0)], ['gs2'])
    CP('dve', sh1, mods3[:, 0:8, 0], ['mods'], ['sh1'])
    CP('dve', sh1c, mods3[:, 0:8, 1], ['mods'], ['sh1c'])
    CP('dve', sh2, mods3[:, 24:32, 0], ['mods'], ['sh2'])
    dg = [A.f32(128) for _ in range(2)]
    for mi, mbase in enumerate((16, 40)):
        for kc in range(8):
            s = kc % 2
            TS('dve', dg[s], ident, mods3[:, mbase + kc, 0:1], None, ALU.mult, None, ['ident', 'mods'], [('dg', s)])
            b = 5 + (kc // 4)
            MM(PS[b][:, (kc % 4) * 128:(kc % 4 + 1) * 128], psk(b), [(ones_f, dg[s], ['ones_f', ('dg', s)])])
        for hf in range(2):
            TS('dve', m25[:, mi * 1024 + hf * 512: mi * 1024 + (hf + 1) * 512], PS[5 + hf], 0.5 if mi == 0 else 1.0, None,
               ALU.mult, None, [psk(5 + hf)], [('m25', mi)])
    lamc = CT[0][:, 66:86]
    ACT(tiny[:, 0:20], lamc, AF.Exp, [('CT', 0)], ['tiny0'], scale=-1.0)
    ACT(tiny[:, 20:40], tiny[:, 0:20], AF.Ln, ['tiny0'], ['tiny1'], bias=1.0)
    TS('dve', cs1, tiny[:, 20:40], -8.0, None, ALU.mult, None, ['tiny1'], ['cs'])
    TS('dve', csh, tiny[:, 20:40], -4.0, None, ALU.mult, None, ['tiny1'], ['cs'])
    TS('dve', hb, CT[0][:, 86:126], 0.5, None, ALU.mult, None, [('CT', 0)], ['hb'])
    stg = A.f32(1024)
    LD(stg, gv_d[0].partition_broadcast(128), ['stg'])
    CP('dve', gvb, stg, ['stg'], ['gvb'])
    LD(gfb, gf_d[0].partition_broadcast(128), ['gfb'])
    wsf = A.f32(8 * 128)
    wsf3 = v3(wsf, 128)
    LD(wsf3, ws_d.rearrange("(h p) q -> p h q", p=128), ['wsf'])
    for h in range(8):
        b = h // 4
        TR(PS[b][:, (h % 4) * 128:(h % 4 + 1) * 128], psk(b), wsf3[:, h, :], ident, ['wsf', 'ident'])
    for b in range(2):
        CP('dve', wsT[:, b * 512:(b + 1) * 512], PS[b], [psk(b)], ['wsT'])
    tr.dma('sp', v3(bsf[0:1, :], 128), bs_d.rearrange("p h -> h p").unsqueeze(0), reads=(), writes=['bsf'],
           semkey='bsf', allow_slow_non_contiguous=True)

    wst = [A.bf16(5632) for _ in range(2)]
    si = 0
    for kc in range(8):
        s = si % 2; si += 1
        parts = ((1280, 3328, 0), (3328, 3584, 2048), (4608, 6656, 2304))
        for pi, (c0, c1, o0) in enumerate(parts):
            tr.dma('pool', wst[s][:, o0:o0 + (c1 - c0)], win_d[kc * 128:(kc + 1) * 128, c0:c1], reads=(),
                   writes=[('wst', s, pi)], semkey=('wst', s, pi))
        STORE(d_win[:, :, kc * 128:(kc + 1) * 128].rearrange("j p c -> p j c"), v3(wst[s][:, 0:4352], 128),
              [('wst', s, pi) for pi in range(3)], ['d_win'])
    for kc in range(18):
        s = si % 2; si += 1
        src = wpr_d[kc * 128:(kc + 1) * 128, :] if kc < 10 else wpg_d[(kc - 10) * 128:(kc - 9) * 128, :]
        tr.dma('pool', wst[s][:, 0:1024], src, reads=(), writes=[('wst', s, pi) for pi in range(3)], semkey=('wst', s, 0))
        STORE(d_wpp[:, :, kc * 128:(kc + 1) * 128].rearrange("j p c -> p j c"), v3(wst[s][:, 0:1024], 128),
              [('wst', s, pi) for pi in range(3)], ['d_wpp'])
    for kc in range(8):
        s = si % 2; si += 1
        for pi, (c0, c1) in enumerate(((0, 2048), (2048, 4096), (4096, 5632))):
            tr.dma('pool', wst[s][:, c0:c1], wup_d[kc * 128:(kc + 1) * 128, c0:c1], reads=(),
                   writes=[('wst', s, pi)], semkey=('wst', s, pi))
        STORE(d_wup[:, :, kc * 128:(kc + 1) * 128].rearrange("j p c -> p j c"), v3(wst[s][:, 0:5632], 128),
              [('wst', s, pi) for pi in range(3)], ['d_wup'])
    STORE(d_pxr[:, :, 0:PADR].rearrange("j p t -> p j t"), v3(zer[:, 0:10 * PADR], PADR), ['zer'], ['d_pxr_pad'])
    STORE(d_pxr[:, :, PADR + T:PADR + T + PADR].rearrange("j p t -> p j t"), v3(zer[:, 0:10 * PADR], PADR), ['zer'], ['d_pxr_pad'])
    if debug and stop_after == 'P0':
        dsm = A.f32(1024)
        tr.op('dve', lambda e: e.memset(dsm, 0.0), (), ['dsm'])
        CP('dve', dsm[:, 0:128], CT[0], [('CT', 0), 'dsm'], ['dsm'])
        CP('dve', dsm[:, 128:224], mods, ['mods', 'dsm'], ['dsm'])
        CP('dve', dsm[:, 224:232], gs1, ['gs1', 'dsm'], ['dsm'])
        CP('dve', dsm[:, 232:240], sh1, ['sh1', 'dsm'], ['dsm'])
        CP('dve', dsm[:, 240:260], cs1, ['cs', 'dsm'], ['dsm'])
        CP('dve', dsm[:, 260:300], hb, ['hb', 'dsm'], ['dsm'])
        CP('dve', dsm[:, 300:428], m25[:, 0:128], [('m25', 0), 'dsm'], ['dsm'])
        CP('dve', dsm[:, 428:556], m25[:, 1024:1152], [('m25', 1), 'dsm'], ['dsm'])
        CP('dve', dsm[:, 556:684], wsT[:, 0:128], ['wsT', 'dsm'], ['dsm'])
        CP('dve', dsm[:, 684:812], bsf[:, 0:128], ['bsf', 'dsm'], ['dsm'])
        STORE(dbg_small, dsm, ['dsm'], ['dbg_small'])
    tr.barrier()
    A.reset(PH0)
    if stop_after == 'P0':
        return finish()

    def norm_to_fm(xt3, xtkey, ntb, gs_t, sh_t, hx3, hxkey, bufs, pbanks, tag, xnkey):
        junk, ssq, rstd, xnr = bufs
        for tb in range(ntb):
            ACT(junk, xt3[:, tb, :], AF.Square, [xtkey], [('junk', tag), ('ssq', tag)], accum_out=ssq[:, tb:tb + 1])
        TS('dve', rstd[:, 0:ntb], ssq[:, 0:ntb], 1.0 / D, EPS, ALU.mult, ALU.add, [('ssq', tag)], [('rstd', tag)])
        TT('pool', rstd[:, 0:ntb], rstd[:, 0:ntb], mh[:, 0:ntb], ALU.pow, [('rstd', tag), 'mh'], [('rstd', tag)])
        N = ntb * 128
        for tb in range(ntb):
            xs = tb % 2
            TS('pool', xnr[xs], xt3[:, tb, :], rstd[:, tb:tb + 1], 1.0, ALU.mult, ALU.mult,
               [xtkey, ('rstd', tag)], [(xnkey, xs)])
            for kc in range(8):
                b = pbanks[kc // 2]
                o = (kc % 2) * 512 + tb * 128
                TR(PSB[b][:, o:o + 128], psk(b), xnr[xs][:, kc * 128:(kc + 1) * 128], identb, [(xnkey, xs), 'identb'])
        for kc in range(8):
            b = pbanks[kc // 2]
            o = (kc % 2) * 512
            if (kc // 2) % 2 == 0:
                ACT(hx3[:, kc, 0:N], PSB[b][:, o:o + N], AF.Identity, [psk(b), gs_t[1], sh_t[1]], [(hxkey, kc)],
                    scale=gs_t[0][:, kc:kc + 1], bias=sh_t[0][:, kc:kc + 1])
            else:
                TS('dve', hx3[:, kc, 0:N], PSB[b][:, o:o + N], gs_t[0][:, kc:kc + 1], sh_t[0][:, kc:kc + 1], ALU.mult, ALU.add,
                   [psk(b), gs_t[1], sh_t[1]], [(hxkey, kc)])

    PXW = CH + 2 * PADR
    PXC = TCTX + 2 * PADR
    pxh = [A.bf16(10 * PXW) for _ in range(2)]
    pxc = A.bf16(10 * PXC)
    PH1 = A.mark()
    wrx = A.bf16(8 * WR)
    wrx3 = v3(wrx, WR)
    for kc in range(8):
        tr.dma('pool', wrx3[:, kc, :], win_d[kc * 128:(kc + 1) * 128, 0:WR], reads=(), writes=[('wrx', kc)], semkey=('wrx', kc))
    xt = [A.f32(4 * 1024) for _ in range(2)]
    junk = A.bf16(1024)
    ssq = [A.f32(4) for _ in range(2)]
    rstd = [A.f32(4) for _ in range(2)]
    xnr = [A.bf16(1024) for _ in range(2)]
    hx = [A.bf16(8 * CH) for _ in range(2)]
    pxo = [A.bf16(10 * CH) for _ in range(2)]

    def p1_chunk(k, s, src_rows, ntb, gs_t, sh_t, store=True):
        N = ntb * 128
        xt3 = v3(xt[s], 1024)
        LD(xt3[:, 0:ntb, :], src_rows.rearrange("(tb p) d -> p tb d", p=128), [('xt', s)])
        hx3 = v3(hx[s], CH)
        norm_to_fm(xt3, ('xt', s), ntb, gs_t, sh_t, hx3, ('hx', s), (junk, ssq[s], rstd[s], xnr), [0, 1, 2, 3], ('p1', s), 'xn1')
        hxk = [(('hx', s), kc) for kc in range(8)]
        po3 = v3(pxo[s], CH)
        for j in range(10):
            b = 4 + j % 4
            MM(PS[b][:, 0:N], psk(b), [(wrx3[:, kc, j * 128:(j + 1) * 128], hx3[:, kc, 0:N], [('wrx', kc), hxk[kc]]) for kc in range(8)])
            if j % 2 == 0:
                ACT(po3[:, j, 0:N], PS[b][:, 0:N], AF.Identity, [psk(b)], [('pxo', s, j)])
            else:
                CP('dve', po3[:, j, 0:N], PS[b][:, 0:N], [psk(b)], [('pxo', s, j)])
        if store:
            STORE(d_hxT[k], hx[s], hxk, [('d_hxT', k)])
            STORE(d_pxr[:, :, PADR + k * CH:PADR + (k + 1) * CH].rearrange("j p t -> p j t"), po3,
                  [('pxo', s, j) for j in range(10)], [('d_pxr', k)])

    p1_chunk(0, 0, ctx_d[:, :], 2, (gs1c, 'gs1c'), (sh1c, 'sh1c'), store=False)
    pxc3 = v3(pxc, PXC)
    tr.op('pool', lambda e: e.memset(pxc, 0.0), (), ['pxc'])
    CP('pool', pxc3[:, :, PADR:PADR + TCTX], v3(pxo[0], CH)[:, :, 0:TCTX], [('pxo', 0, j) for j in range(10)] + ['pxc'], ['pxc'])
    for k in range(NCH):
        p1_chunk(k, (k + 1) % 2, x_d[k * CH:(k + 1) * CH, :], 4, (gs1, 'gs1'), (sh1, 'sh1'))
    tr.barrier()
    A.reset(PH1)
    if stop_after == 'P1':
        return finish()

    dg1 = A.bf16(40 * 128)
    dg13 = v3(dg1, 128)
    for kc in range(10):
        for tap in range(4):
            TS('pool', dg13[:, kc * 4 + tap, :], ident, cwc(tap, kc), 1.0, ALU.mult, ALU.mult, ['ident', ('CT', 0)], ['dg1'])
    wg = A.bf16(2 * 5 * 2 * 256)
    wg5 = wg.rearrange("p (g h i j) -> p g h i j", g=2, h=5, i=2)

    def load_wg(d):
        for gi, wsrc in enumerate((wa_d, wx_d)):
            tr.dma('pool', wg5[:, gi], wsrc[d].rearrange("h (i p) j -> p h i j", p=128), reads=(), writes=[('wg', gi)], semkey=('wg', gi))

    xr_s = [A.bf16(10 * CH) for _ in range(2)]
    r_s = [A.bf16(5 * CH) for _ in range(2)]; i_s = [A.bf16(5 * CH) for _ in range(2)]
    a_s = [A.f32(5 * CH) for _ in range(2)]; e2_s = [A.f32(5 * CH) for _ in range(2)]
    m_s = [A.bf16(5 * CH) for _ in range(2)]
    lru_n = [0]
    yo = [A.bf16(10 * CH) for _ in range(2)]
    ybl = [A.bf16(10 * CH) for _ in range(2)]
    hst3 = v3(hstate, 10)

    def rev(ap3, kc, N):
        return bass.AP(ap3.tensor, ap3[:, kc, N - 1:N].offset, [list(ap3.ap[0]), [-1, N]])

    def lru_chunk(d, N, ph3, phkeys, y3, ykey, init_of, reverse):
        xs = lru_n[0] % 2
        lru_n[0] += 1
        xr3 = v3(xr_s[xs], CH)
        XR = lambda kc: ('xr', xs, kc)
        for kc in range(10):
            b = kc % 4
            MM(PS[b][:, 0:N], psk(b),
               [(dg13[:, kc * 4 + tap, :], ph3[:, kc, PADR + tap - 2:PADR + tap - 2 + N], ['dg1'] + phkeys) for tap in range(4)])
            if kc % 2 == 0:
                ACT(xr3[:, kc, 0:N], PS[b][:, 0:N], AF.Identity, [psk(b), ('CT', 0)], [XR(kc)], bias=cbc(kc))
            else:
                TS('dve', xr3[:, kc, 0:N], PS[b][:, 0:N], cbc(kc), None, ALU.add, None, [psk(b), ('CT', 0)], [XR(kc)])
        for hf in range(2):
            kcs = list(range(hf * 5, hf * 5 + 5))
            r3 = v3(r_s[hf], CH); i3 = v3(i_s[hf], CH); a3 = v3(a_s[hf], CH); e23 = v3(e2_s[hf], CH); m3 = v3(m_s[hf], CH)
            for kc in kcs:
                h, jc, q = kc // 2, kc % 2, kc - hf * 5
                for gi, dst3, nm in ((0, r3, 'r'), (1, i3, 'i')):
                    b = 4 + (kc * 2 + gi) % 4
                    MM(PS[b][:, 0:N], psk(b),
                       [(wg5[:, gi, h, ic, jc * 128:(jc + 1) * 128], xr3[:, 2 * h + ic, 0:N], [('wg', gi), XR(2 * h + ic)]) for ic in range(2)])
                    ACT(dst3[:, q, 0:N], PS[b][:, 0:N], AF.Tanh, [psk(b), 'hb'], [(nm, hf, q)], scale=0.5, bias=hbc(gi, d, kc))
            for kc in kcs:
                q = kc - hf * 5
                cc = d * 10 + kc
                ACT(a3[:, q, 0:N], r3[:, q, 0:N], AF.Exp, [('r', hf, q), 'cs'], [('a', hf, q)], scale=csh[:, cc:cc + 1], bias=csh[:, cc:cc + 1])
                ACT(e23[:, q, 0:N], r3[:, q, 0:N], AF.Exp, [('r', hf, q), 'cs'], [('e2', hf, q)], scale=cs1[:, cc:cc + 1], bias=cs1[:, cc:cc + 1])
            for kc in kcs:
                q = kc - hf * 5
                ACT(m3[:, q, 0:N], e23[:, q, 0:N], AF.Sqrt, [('e2', hf, q)], [('m', hf, q)], scale=-0.25, bias=0.25)
            for kc in kcs:
                q = kc - hf * 5
                STT(i3[:, q, 0:N], i3[:, q, 0:N], 1.0, xr3[:, kc, 0:N], ALU.add, ALU.mult, [('i', hf, q), XR(kc)], [('i', hf, q)])
                TT('dve', i3[:, q, 0:N], i3[:, q, 0:N], m3[:, q, 0:N], ALU.mult, [('i', hf, q), ('m', hf, q)], [('i', hf, q)])
                init_ap, init_keys = init_of(kc)
                if reverse:
                    o, aa, bb = rev(y3, kc, N), rev(a3, q, N), rev(i3, q, N)
                else:
                    o, aa, bb = y3[:, kc, 0:N], a3[:, q, 0:N], i3[:, q, 0:N]
                tr.op('dve', lambda e, o=o, aa=aa, bb=bb, init_ap=init_ap: e.tensor_tensor_scan(
                    out=o, data0=aa, data1=bb, initial=init_ap, op0=ALU.mult, op1=ALU.add),
                    reads=[('a', hf, q), ('i', hf, q)] + init_keys, writes=[(ykey, kc)])

    def ctx_pass(d):
        y3 = v3(yo[d], CH)
        lru_chunk(d, TCTX, pxc3, ['pxc'], y3, ('yo', d), lambda kc: (0.0, []), reverse=(d == 1))
        col = TCTX - 1 if d == 0 else 0
        CP('dve', hst3[:, d, :], y3[:, :, col], [(('yo', d), kc) for kc in range(10)], [('hst', d)])

    def lru_pass(d, order, post):
        prev = None

        def ld_px(n_it):
            k = order[n_it]
            s = n_it % 2
            LD(v3(pxh[s], PXW), d_pxr[:, :, k * CH:k * CH + PXW].rearrange("j p t -> p j t"), [('pxh', s)],
               reads=[('d_pxr', kk) for kk in (k - 1, k, k + 1) if 0 <= kk < NCH] + ['d_pxr_pad'])

        ld_px(0)
        for n_it, k in enumerate(order):
            s = n_it % 2
            ph3 = v3(pxh[s], PXW)
            if n_it + 1 < len(order):
                ld_px(n_it + 1)
            y3 = v3(yo[s], CH)
            if prev is None:
                init_of = lambda kc: (hst3[:, d, kc:kc + 1], [('hst', d)])
            else:
                py3 = v3(yo[prev], CH)
                pcol = 0 if d == 1 else CH - 1
                init_of = lambda kc, py3=py3, pcol=pcol, prev=prev: (py3[:, kc, pcol:pcol + 1], [(('yo', prev), kc)])
            lru_chunk(d, CH, ph3, [('pxh', s)], y3, ('yo', s), init_of, reverse=(d == 1))
            post(k, s, y3)
            prev = s

    def post_bwd(k, s, y3):
        STORE(d_yb[k], yo[s], [(('yo', s), kc) for kc in range(10)], [('d_yb', k)])

    def post_fwd(k, s, y3):
        LD(ybl[s], d_yb[k], [('ybl', s)], reads=[('d_yb', k)])
        yb3 = v3(ybl[s], CH)
        for kc in range(10):
            TT('pool', yb3[:, kc, :], y3[:, kc, :], yb3[:, kc, :], ALU.add, [(('yo', s), kc), ('ybl', s)], [('ybl', s)])
        STORE(d_yl[k], ybl[s], [('ybl', s)], [('d_yl', k)])

    load_wg(1)
    ctx_pass(1)
    if stop_after == 'LRUc':
        if debug:
            dsm = A.f32(1024)
            tr.op('dve', lambda e: e.memset(dsm, 0.0), (), ['dsm'])
            CP('dve', dsm[:, 0:20], hstate, [('hst', 1), 'dsm'], ['dsm'])
            CP('dve', dsm[:, 32:32 + 256], v3(yo[1], CH)[:, 3, 0:256], [(('yo', 1), 3), 'dsm'], ['dsm'])
            CP('dve', dsm[:, 320:320 + 256], xr3[:, 3, 0:256], [('xr', 3), 'dsm'], ['dsm'])
            STORE(dbg_small, dsm, ['dsm'], ['dbg_small'])
        tr.barrier()
        return finish()
    if stop_after == 'LRUb2':
        lru_pass(1, [NCH - 1, NCH - 2], post_bwd)
        tr.barrier()
        return finish()
    lru_pass(1, list(range(NCH - 1, -1, -1)), post_bwd)
    if stop_after == 'LRUb':
        tr.barrier()
        return finish()
    load_wg(0)
    ctx_pass(0)
    lru_pass(0, list(range(NCH)), post_fwd)
    tr.barrier()
    A.reset(PH0)
    if stop_after == 'LRU':
        return finish()

    wv = A.bf16(8 * 1024); wv3 = v3(wv, 1024)
    for kc in range(8):
        tr.dma('pool', wv3[:, kc, :], win_d[kc * 128:(kc + 1) * 128, 3584:4608], reads=(), writes=[('wv', kc)], semkey=('wv', kc))
    wvk = [('wv', kc) for kc in range(8)]
    wo = A.bf16(8 * 1024); wo3 = v3(wo, 1024)
    PHM = A.mark()
    wstg = A.f32(8 * 1024); wstg3 = v3(wstg, 1024)
    LD(wstg3, wout_d.rearrange("(kc p) n -> p kc n", p=128), ['wstg'])
    for kc in range(8):
        TT('dve', wo3[:, kc, :], wstg3[:, kc, :], m25[:, 0:1024], ALU.mult, ['wstg', ('m25', 0)], ['wo'])
    tr.barrier()
    A.reset(PHM)
    ring = [A.bf16(1024) for _ in range(4)]
    ppr = [A.bf16(18 * 128) for _ in range(2)]
    hxm = [A.bf16(8 * CH) for _ in range(2)]
    ylm = [A.bf16(10 * CH) for _ in range(2)]
    gtmp = [A.bf16(CH) for _ in range(2)]
    u_t = A.bf16(8 * CH)
    gvt = [A.bf16(1024) for _ in range(2)]
    vpp = A.bf16(4 * 1024)
    yg = A.bf16(8 * CH)
    gate = [A.bf16(CH) for _ in range(4)]
    t1 = [A.bf16(CH) for _ in range(2)]
    t2 = [A.bf16(CH) for _ in range(2)]
    mrg = A.bf16(8 * CH)
    xm = A.f32(4 * 1024)
    ssv = [A.f32(1) for _ in range(2)]
    ssq2 = A.f32(4); rstd2 = A.f32(4)
    xnr2 = [A.bf16(1024) for _ in range(2)]
    h2o = A.bf16(8 * CH)
    junk2 = A.bf16(1024)
    ringn = [0]

    def stream_slice(js):
        s = ringn[0] % 4
        ringn[0] += 1
        LD(ring[s], d_win[js], [('ring', s)], reads=['d_win'])
        return v3(ring[s], 128), ('ring', s)

    def m_loads(k):
        sb = k % 2
        LD(hxm[sb], d_hxT[k], [('hxm', sb)], reads=[('d_hxT', k)])
        LD(ylm[sb], d_yl[k], [('ylm', sb)], reads=[('d_yl', k)])

    def m_chunk(k):
        sb = k % 2
        HXK = ('hxm', sb); YLK = ('ylm', sb)
        hx3 = v3(hxm[sb], CH)
        xm3 = v3(xm, 1024)
        LD(xm3, x_d[k * CH:(k + 1) * CH, :].rearrange("(tb p) d -> p tb d", p=128), ['xm'])
        if k + 1 < NCH:
            m_loads(k + 1)
        yl3 = v3(ylm[sb], CH); u3 = v3(u_t, CH); yg3 = v3(yg, CH); mg3 = v3(mrg, CH)
        hk = [HXK]
        for j in range(10):
            w3, wk = stream_slice(j)
            b = j % 4
            MM(PS[b], psk(b), [(w3[:, kc, :], hx3[:, kc, :], [wk] + hk) for kc in range(8)])
            ACT(gtmp[j % 2], PS[b], AF.Gelu_apprx_tanh, [psk(b)], [('gtmp', j % 2)])
            TT('pool', yl3[:, j, :], yl3[:, j, :], gtmp[j % 2], ALU.mult, [YLK, ('gtmp', j % 2)], [YLK])
        for j in range(8):
            w3, wk = stream_slice(10 + j)
            b = (10 + j) % 4
            MM(PS[b], psk(b), [(w3[:, kc, :], hx3[:, kc, :], [wk] + hk) for kc in range(8)])
            ACT(u3[:, j, :], PS[b], AF.Gelu_apprx_tanh, [psk(b)], [('u', j)])
        vp3 = v3(vpp, 1024)
        for tb in range(4):
            gs_ = tb % 2
            for hf in range(2):
                b = 4 + (tb * 2 + hf) % 2
                MM(PS[b], psk(b), [(hx3[:, kc, tb * 128:(tb + 1) * 128], wv3[:, kc, hf * 512:(hf + 1) * 512], hk + [wvk[kc]]) for kc in range(8)])
                ACT(gvt[gs_][:, hf * 512:(hf + 1) * 512], PS[b], AF.Gelu_apprx_tanh, [psk(b)], [('gvt', gs_, hf)])
            gk = [('gvt', gs_, 0), ('gvt', gs_, 1)]
            ACT(junk2, gvt[gs_], AF.Square, gk, ['junk2v', ('ssv', gs_)], accum_out=ssv[gs_])
            TS('dve', ssv[gs_], ssv[gs_], 1.0 / 1024, EPS, ALU.mult, ALU.add, [('ssv', gs_)], [('ssv', gs_)])
            TT('pool', ssv[gs_], ssv[gs_], mh[:, 0:1], ALU.pow, [('ssv', gs_), 'mh'], [('ssv', gs_)])
            STT(vp3[:, tb, :], gvt[gs_], ssv[gs_], gvb, ALU.mult, ALU.mult, gk + [('ssv', gs_), 'gvb'], [('vpp', tb)])
        wsT3 = v3(wsT, 128); bsf3 = v3(bsf[0:1, :], 128)
        for h in range(8):
            b = 6 + h % 2
            for tb in range(4):
                MM(PS[b][:, tb * 128:(tb + 1) * 128], psk(b),
                   [(vp3[:, tb, h * 128:(h + 1) * 128], wsT3[:, h, :], [('vpp', tb), 'wsT']),
                    (ones_f[0:1, :], bsf3[0:1, h, :], ['ones_f', 'bsf'])])
            TT('dve', yg3[:, h, :], PS[b], u3[:, h, :], ALU.mult, [psk(b), ('u', h)], [('yg', h)])
        for j in range(8):
            sp_ = j % 2
            LD(ppr[sp_], d_wpp[j], [('ppr', sp_)], reads=['d_wpp'])
            pp3 = v3(ppr[sp_], 128)
            for gi in range(2):
                w3, wk = stream_slice(18 + gi * 8 + j)
                b = (j * 2 + gi) % 4
                MM(PS[b], psk(b), [(w3[:, kc, :], hx3[:, kc, :], [wk] + hk) for kc in range(8)])
                ACT(gate[sp_ * 2 + gi], PS[b], AF.Tanh, [psk(b)], [('gate', sp_ * 2 + gi)], scale=0.5)
            b1 = 4 + (j * 2) % 4
            MM(PS[b1], psk(b1), [(pp3[:, kc, :], yl3[:, kc, :], [('ppr', sp_), YLK]) for kc in range(10)])
            STT(t1[sp_], gate[sp_ * 2], 1.0, PS[b1], ALU.add, ALU.mult, [psk(b1), ('gate', sp_ * 2)], [('t1', sp_)])
            b2 = 4 + (j * 2 + 1) % 4
            MM(PS[b2], psk(b2), [(pp3[:, 10 + kc, :], yg3[:, kc, :], [('ppr', sp_), ('yg', kc)]) for kc in range(8)])
            STT(t2[sp_], gate[sp_ * 2 + 1], 1.0, PS[b2], ALU.add, ALU.mult, [psk(b2), ('gate', sp_ * 2 + 1)], [('t2', sp_)])
            TT('pool', mg3[:, j, :], t1[sp_], t2[sp_], ALU.add, [('t1', sp_), ('t2', sp_)], [('mrg', j)])
        mk = [('mrg', j) for j in range(8)]
        for tb in range(4):
            for hf in range(2):
                b = (tb * 2 + hf) % 4
                MM(PS[b], psk(b), [(mg3[:, kc, tb * 128:(tb + 1) * 128], wo3[:, kc, hf * 512:(hf + 1) * 512], [mk[kc], 'wo']) for kc in range(8)])
                TT('dve', xm3[:, tb, hf * 512:(hf + 1) * 512], PS[b], xm3[:, tb, hf * 512:(hf + 1) * 512], ALU.add, [psk(b), 'xm'], ['xm'])
        STORE(d_x1[k * CH:(k + 1) * CH, :].rearrange("(tb p) d -> p tb d", p=128), xm3, ['xm'], [('d_x1', k)])
        h23 = v3(h2o, CH)
        norm_to_fm(xm3, 'xm', 4, (gs2, 'gs2'), (sh2, 'sh2'), h23, 'h2o', (junk2, ssq2, rstd2, xnr2), [4, 5, 6, 7], 'm', 'xn2')
        STORE(d_h2T[:, :, PADH + k * CH:PADH + (k + 1) * CH].rearrange("kc p t -> p kc t"), h23,
              [('h2o', kc) for kc in range(8)], [('d_h2T', k)])

    m_loads(0)
    for k in range(NCH):
        m_chunk(k)
    tr.barrier()
    A.reset(PH0)
    if stop_after == 'M':
        return finish()

    wd = A.bf16(NJ * 1024); wd3 = v3(wd, 1024)
    PHF = A.mark()
    wstg = A.f32(8 * 1024); wstg3 = v3(wstg, 1024)
    for g0 in range(0, NJ, 8):
        gn = min(8, NJ - g0)
        LD(wstg3[:, 0:gn, :], wdn_d[g0 * 128:(g0 + gn) * 128, :].rearrange("(kc p) n -> p kc n", p=128), ['wstg'])
        for kc in range(gn):
            TT('dve', wd3[:, g0 + kc, :], wstg3[:, kc, :], m25[:, 1024:2048], ALU.mult, ['wstg', ('m25', 1)], ['wd'])
    tr.barrier()
    A.reset(PHF)
    dg2 = A.bf16(NJ * 9 * 128)
    dg2v = v3(dg2, 128)
    for j in range(NJ):
        for tap in range(9):
            TS('pool', dg2v[:, j * 9 + tap, :], ident, fcwc(tap, j), 1.0, ALU.mult, ALU.mult, ['ident', ('CT', 1), ('CT', 2)], ['dg2'])
    ringf = [A.bf16(1024) for _ in range(4)]
    HW_ = CH + 2 * PADH
    h2h = [A.bf16(8 * HW_) for _ in range(2)]
    gbuf = [A.bf16(HW_) for _ in range(2)]
    ggl = [A.bf16(CH) for _ in range(2)]
    actb = A.bf16(NJ * CH)
    x1t = A.f32(4 * 1024)
    ssq3 = A.f32(4); rstd3 = A.f32(4)
    junk3 = A.bf16(1024)
    outt = [A.f32(1024) for _ in range(2)]
    ringfn = [0]

    def f_loads(k):
        sb = k % 2
        h3 = v3(h2h[sb], HW_)
        lo = k * CH
        c0 = PADH if k == 0 else 0
        c1 = HW_ - PADH if k == NCH - 1 else HW_
        LD(h3[:, :, c0:c1], d_h2T[:, :, lo + c0:lo + c1].rearrange("kc p t -> p kc t"), [('h2h', sb)],
           reads=[('d_h2T', kk) for kk in (k - 1, k, k + 1) if 0 <= kk < NCH])

    def f_chunk(k):
        sb = k % 2
        H2K = ('h2h', sb)
        h3 = v3(h2h[sb], HW_)
        x13 = v3(x1t, 1024)
        LD(x13, d_x1[k * CH:(k + 1) * CH, :].rearrange("(tb p) d -> p tb d", p=128), ['x1t'], reads=[('d_x1', k)])
        if k + 1 < NCH:
            f_loads(k + 1)
        ab3 = v3(actb, CH)
        for j in range(NJ):
            sg = j % 2
            d3 = dg2v[:, j * 9:(j + 1) * 9, :]
            sl = []
            for part in range(2):
                s = ringfn[0] % 4
                ringfn[0] += 1
                LD(ringf[s], d_wup[part * NJ + j], [('ringf', s)], reads=['d_wup'])
                sl.append((v3(ringf[s], 128), ('ringf', s)))
            gb = gbuf[sg]
            w3, wk = sl[0]
            bm = (j % 2) * 4
            MM(PS[bm], psk(bm), [(w3[:, kc, :], h3[:, kc, PADH:PADH + CH], [wk, H2K]) for kc in range(8)])
            ACT(gb[:, PADH:PADH + CH], PS[bm], AF.Identity, [psk(bm)], [('gb', sg, 1)])
            bh = bm + 1
            if k > 0:
                MM(PS[bh][:, 0:PADH], psk(bh), [(w3[:, kc, :], h3[:, kc, 0:PADH], [wk, H2K]) for kc in range(8)])
            if k < NCH - 1:
                MM(PS[bh][:, PADH:2 * PADH], psk(bh), [(w3[:, kc, :], h3[:, kc, PADH + CH:HW_], [wk, H2K]) for kc in range(8)])
            if k > 0:
                CP('dve', gb[:, 0:PADH], PS[bh][:, 0:PADH], [psk(bh)], [('gb', sg, 0)])
            else:
                tr.op('dve', lambda e, gb=gb: e.memset(gb[:, 0:PADH], 0.0), [], [('gb', sg, 0)])
            if k < NCH - 1:
                CP('dve', gb[:, PADH + CH:HW_], PS[bh][:, PADH:2 * PADH], [psk(bh)], [('gb', sg, 2)])
            else:
                tr.op('dve', lambda e, gb=gb: e.memset(gb[:, PADH + CH:HW_], 0.0), [], [('gb', sg, 2)])
            w3v, wkv = sl[1]
            bv = bm + 3
            MM(PS[bv], psk(bv), [(w3v[:, kc, :], h3[:, kc, PADH:PADH + CH], [wkv, H2K]) for kc in range(8)])
            gbk = [('gb', sg, i_) for i_ in range(3)]
            bc = bm + 2
            ops = []
            for dr in (0, -1, 1):
                tap = (dr + 1) * 3 + 1
                ops.append((d3[:, tap, :], gb[:, PADH + dr * 64:PADH + dr * 64 + CH], PS[bc]))
            pso = v3(PS[bc], 64)
            for dr in (0, -1, 1):
                src = v3(gb[:, PADH + dr * 64:PADH + dr * 64 + CH], 64)
                ops.append((d3[:, (dr + 1) * 3 + 0, :], src[:, :, 0:63], pso[:, :, 1:64]))
                ops.append((d3[:, (dr + 1) * 3 + 2, :], src[:, :, 1:64], pso[:, :, 0:63]))
            for i_, (l, r, o) in enumerate(ops):
                tr.op('pe', lambda e, l=l, r=r, o=o, i_=i_: e.matmul(o, l, r, start=(i_ == 0), stop=(i_ == 8)),
                      reads=['dg2'] + gbk, writes=[psk(bc)])
            ACT(ggl[sg], PS[bc], AF.Gelu_apprx_tanh, [psk(bc), ('CT', 2)], [('ggl', sg)], bias=fcbc(j))
            TT('dve', ab3[:, j, :], PS[bv], ggl[sg], ALU.mult, [psk(bv), ('ggl', sg)], [('actb', j)])
        ak = [('actb', j) for j in range(NJ)]
        for tb in range(4):
            for hf in range(2):
                b = (tb * 2 + hf) % 4
                MM(PS[b], psk(b), [(ab3[:, j, tb * 128:(tb + 1) * 128], wd3[:, j, hf * 512:(hf + 1) * 512], [ak[j], 'wd']) for j in range(NJ)])
                TT('dve', x13[:, tb, hf * 512:(hf + 1) * 512], PS[b], x13[:, tb, hf * 512:(hf + 1) * 512], ALU.add, [psk(b), 'x1t'], ['x1t'])
            ACT(junk3, x13[:, tb, :], AF.Square, ['x1t'], ['junk3', ('ssq3', tb)], accum_out=ssq3[:, tb:tb + 1])
            TS('dve', rstd3[:, tb:tb + 1], ssq3[:, tb:tb + 1], 1.0 / D, EPS, ALU.mult, ALU.add, [('ssq3', tb)], [('rstd3', tb)])
            TT('pool', rstd3[:, tb:tb + 1], rstd3[:, tb:tb + 1], mh[:, 0:1], ALU.pow, [('rstd3', tb), 'mh'], [('rstd3', tb)])
            so = tb % 2
            STT(outt[so], x13[:, tb, :], rstd3[:, tb:tb + 1], gfb, ALU.mult, ALU.mult, ['x1t', ('rstd3', tb), 'gfb'], [('outt', so)])
            STORE(out_d[k * CH + tb * 128:k * CH + (tb + 1) * 128, :], outt[so], [('outt', so)], [('out', k, tb)])

    f_loads(0)
    for k in range(NCH):
        f_chunk(k)
    return finish()


_INPUT_ORDER = ["x", "c", "ctx", "c_ctx", "w_mod", "b_mod", "g_norm1", "g_norm2", "w_in", "conv_w", "conv_b",
                "lru_lam", "lru_ba", "lru_bx", "lru_wa", "lru_wx", "g_v", "w_s", "b_s", "w_pr", "w_pg", "w_out",
                "w_up", "ffn_conv_w", "ffn_conv_b", "w_down", "g_final"]


def make_in_maps(inputs, cores):
    f = lambda a: np.ascontiguousarray(np.asarray(a, dtype=np.float32))
    sh = {
        "w_mod": f(inputs["w_mod"][0]), "b_mod": f(inputs["b_mod"][0]).reshape(48, 128),
        "g_norm1": f(inputs["g_norm1"][0]).reshape(8, 128), "g_norm2": f(inputs["g_norm2"][0]).reshape(8, 128),
        "w_in": f(inputs["w_in"][0]), "conv_w": f(inputs["conv_w"][0]).reshape(40, 128),
        "conv_b": f(inputs["conv_b"][0]).reshape(10, 128), "lru_lam": f(inputs["lru_lam"][0]).reshape(20, 128),
        "lru_ba": f(inputs["lru_ba"][0]).reshape(20, 128), "lru_bx": f(inputs["lru_bx"][0]).reshape(20, 128),
        "lru_wa": f(inputs["lru_wa"][0]), "lru_wx": f(inputs["lru_wx"][0]),
        "g_v": f(inputs["g_v"][0]).reshape(1, D), "w_s": f(inputs["w_s"][0]).reshape(8 * 128, 128),
        "b_s": f(inputs["b_s"][0]), "w_pr": f(inputs["w_pr"][0]), "w_pg": f(inputs["w_pg"][0]),
        "w_out": f(inputs["w_out"][0]), "w_up": f(inputs["w_up"][0]),
        "ffn_conv_w": f(inputs["ffn_conv_w"][0]).reshape(198, 128), "ffn_conv_b": f(inputs["ffn_conv_b"][0]).reshape(22, 128),
        "w_down": f(inputs["w_down"][0]), "g_final": f(inputs["g_final"]).reshape(1, D),
        "c_ctx": f(inputs["c_ctx"]).reshape(8, 128),
    }
    maps = []
    for b in cores:
        m = dict(sh)
        m["x"] = f(inputs["x"][b]); m["c"] = f(inputs["c"][b]).reshape(8, 128); m["ctx"] = f(inputs["ctx"][b])
        maps.append(m)
    return maps


def kernel(**inputs):
    nc = build_program(debug=False)
    maps = make_in_maps(inputs, list(range(8)))
    res = run_bass_kernel_spmd(nc, maps, core_ids=list(range(8)))
    out = np.stack([np.asarray(r["out"], dtype=np.float32).reshape(T, D) for r in res.results], axis=0)
    return out
```

```python
import numpy as np
from contextlib import ExitStack
import concourse.bass as bass
import concourse.mybir as mybir
from concourse.bass_utils import run_bass_kernel_spmd
from concourse.alu_op_type import AluOpType as ALU

F32 = mybir.dt.float32
BF16 = mybir.dt.bfloat16
I32 = mybir.dt.int32
AF = mybir.ActivationFunctionType

D = 1024
T = 8192
CH = 512
NCH = T // CH
WR = 1280
NIN = 6656
DFF = 2816
NJ = DFF // 128
TCTX = 256
EPS = 1e-6
ARENA_WORDS = 52000
PADR = 8
PADH = 64


class Trk:
    def __init__(self, nc, es):
        self.nc = nc
        self.es = es
        self.eng = {'pe': nc.tensor, 'act': nc.scalar, 'dve': nc.vector, 'pool': nc.gpsimd, 'sp': nc.sync}
        self.semh = {}
        self.cnt = {}
        for e in ['pe', 'act', 'dve', 'pool']:
            self.semh[e] = es.enter_context(nc.semaphore("sem_" + e))
            self.cnt[e] = 0
        self.known = {e: {} for e in self.eng}
        self.lastw = {}
        self.readers = {}
        self.ndsem = 0
        self.dkey2sem = {}

    def _wait(self, e, ev):
        name, val, src = ev
        if self.known[e].get(name, 0) >= val:
            return
        self.eng[e].wait_ge(self.semh[name], val)
        self.known[e][name] = val

    def _deps(self, e, reads, writes):
        for k in reads:
            ev = self.lastw.get(k)
            if ev is not None:
                self._wait(e, ev)
        for k in writes:
            ev = self.lastw.get(k)
            if ev is not None and ev[2] != e:
                self._wait(e, ev)
            rd = self.readers.get(k)
            if rd:
                for name, (val, src) in rd.items():
                    if src != e:
                        self._wait(e, (name, val, src))

    def _record(self, ev, reads, writes):
        for k in reads:
            rd = self.readers.setdefault(k, {})
            old = rd.get(ev[0])
            if old is None or old[0] < ev[1]:
                rd[ev[0]] = (ev[1], ev[2])
        for k in writes:
            self.lastw[k] = ev
            self.readers[k] = {}

    def op(self, e, fn, reads=(), writes=()):
        xr = [k for k in reads if isinstance(k, tuple) and k[0] == 'ps']
        if xr:
            writes = list(writes) + xr
        self._deps(e, reads, writes)
        ins = fn(self.eng[e])
        self.cnt[e] += 1
        ins.then_inc(self.semh[e], 1)
        ev = (e, self.cnt[e], e)
        self._record(ev, reads, writes)
        return ev

    def dma(self, q, out, in_, reads=(), writes=(), semkey=None, **kw):
        self._deps(q, reads, writes)
        name = self.dkey2sem.get(semkey)
        if name is None:
            name = "dma%d" % self.ndsem
            self.ndsem += 1
            self.semh[name] = self.es.enter_context(self.nc.semaphore(name))
            self.cnt[name] = 0
            self.dkey2sem[semkey] = name
        ins = self.eng[q].dma_start(out=out, in_=in_, **kw)
        self.cnt[name] += 16
        ins.then_inc(self.semh[name], 16)
        ev = (name, self.cnt[name], None)
        self._record(ev, reads, writes)
        return ev

    def barrier(self, engines=('pe', 'act', 'dve', 'pool', 'sp')):
        for e in engines:
            for name, c in self.cnt.items():
                if c > 0 and name != e:
                    self._wait(e, (name, c, None))


class Arena:
    def __init__(self, ap):
        self.ap = ap
        self.off = 0
        self.n = ap.shape[1]

    def f32(self, n):
        o = self.off
        self.off += n
        assert self.off <= self.n, ("arena overflow", self.off, self.n)
        return self.ap[:, o:o + n]

    def bf16(self, n):
        w = (n + 1) // 2
        return self.f32(w).bitcast(BF16)[:, 0:n]

    def mark(self):
        return self.off

    def reset(self, m):
        self.off = m


def v3(ap, b):
    return ap.rearrange("p (a b) -> p a b", b=b)


def build_program(debug=False, stop_after=None, dbg_names=()):
    nc = bass.Bass("TRN2", target_bir_lowering=False)
    es = ExitStack()

    def din(name, shape, dt=F32):
        return nc.dram_tensor(name, list(shape), dt, kind="ExternalInput").ap()

    def dscr(name, shape, dt):
        kind = "ExternalOutput" if (debug and name in dbg_names) else "Internal"
        return nc.dram_tensor(name, list(shape), dt, kind=kind).ap()

    x_d = din("x", [T, D]); c_d = din("c", [8, 128]); ctx_d = din("ctx", [TCTX, D]); cctx_d = din("c_ctx", [8, 128])
    wmod_d = din("w_mod", [D, 6 * D]); bmod_d = din("b_mod", [48, 128])
    g1_d = din("g_norm1", [8, 128]); g2_d = din("g_norm2", [8, 128])
    win_d = din("w_in", [D, NIN]); cw_d = din("conv_w", [40, 128]); cb_d = din("conv_b", [10, 128])
    lam_d = din("lru_lam", [20, 128]); ba_d = din("lru_ba", [20, 128]); bx_d = din("lru_bx", [20, 128])
    wa_d = din("lru_wa", [2, 5, 256, 256]); wx_d = din("lru_wx", [2, 5, 256, 256])
    gv_d = din("g_v", [1, D]); ws_d = din("w_s", [8 * 128, 128]); bs_d = din("b_s", [128, 8])
    wpr_d = din("w_pr", [WR, D]); wpg_d = din("w_pg", [D, D]); wout_d = din("w_out", [D, D])
    wup_d = din("w_up", [D, 2 * DFF]); fcw_d = din("ffn_conv_w", [198, 128]); fcb_d = din("ffn_conv_b", [22, 128])
    wdn_d = din("w_down", [DFF, D]); gf_d = din("g_final", [1, D])
    out_d = nc.dram_tensor("out", [T, D], F32, kind="ExternalOutput").ap()

    d_hxT = dscr("d_hxT", [NCH, 128, 8 * CH], BF16)
    d_pxr = dscr("d_pxr", [10, 128, T + 2 * PADR], BF16)
    d_yb = dscr("d_yb", [NCH, 128, 10 * CH], BF16)
    d_yl = dscr("d_yl", [NCH, 128, 10 * CH], BF16)
    d_x1 = dscr("d_x1", [T, D], F32)
    d_h2T = dscr("d_h2T", [8, 128, T + 2 * PADH], BF16)
    d_win = dscr("d_win", [34, 128, 1024], BF16)
    d_wpp = dscr("d_wpp", [8, 128, 18 * 128], BF16)
    d_wup = dscr("d_wup", [44, 128, 1024], BF16)

    arena_t = es.enter_context(nc.sbuf_tensor("arena", [128, ARENA_WORDS], F32))
    A = Arena(arena_t[:, :])
    ps_t = es.enter_context(nc.psum_tensor("ps", [128, 8 * 512], F32))
    PS = [ps_t[:, b * 512:(b + 1) * 512] for b in range(8)]
    PSB = [p.bitcast(BF16) for p in PS]
    tr = Trk(nc, es)

    def psk(b):
        return ('ps', b)

    def finish():
        tr.barrier(('sp',))
        es.close()
        return nc

    dbg_small = nc.dram_tensor('dbg_small', [128, 1024], F32, kind='ExternalOutput').ap() if (debug and stop_after in ('P0', 'LRUc')) else None

    def ACT(out, in_, func, reads, writes, bias=None, scale=None, accum_out=None):
        kw = {}
        if bias is not None:
            kw['bias'] = bias
        if scale is not None:
            kw['scale'] = scale
        if accum_out is not None:
            kw['accum_out'] = accum_out
        return tr.op('act', lambda e: e.activation(out=out, in_=in_, func=func, **kw), reads, writes)

    def TS(eng, out, in0, s1, s2, op0, op1, reads, writes):
        if s2 is None:
            return tr.op(eng, lambda e: e.tensor_scalar(out=out, in0=in0, scalar1=s1, scalar2=None, op0=op0), reads, writes)
        return tr.op(eng, lambda e: e.tensor_scalar(out=out, in0=in0, scalar1=s1, scalar2=s2, op0=op0, op1=op1), reads, writes)

    def TT(eng, out, in0, in1, op, reads, writes):
        return tr.op(eng, lambda e: e.tensor_tensor(out=out, in0=in0, in1=in1, op=op), reads, writes)

    def STT(out, in0, scalar, in1, op0, op1, reads, writes):
        return tr.op('dve', lambda e: e.scalar_tensor_tensor(out=out, in0=in0, scalar=scalar, in1=in1, op0=op0, op1=op1), reads, writes)

    def CP(eng, out, in_, reads, writes):
        return tr.op(eng, lambda e: e.tensor_copy(out=out, in_=in_), reads, writes)

    def MM(ps_ap, pskey, ops):
        n = len(ops)
        for i, (l, r, ks) in enumerate(ops):
            tr.op('pe', lambda e, l=l, r=r, i=i: e.matmul(ps_ap, l, r, start=(i == 0), stop=(i == n - 1)),
                  reads=ks, writes=[pskey])

    def TR(out_ap, pskey, in_ap, ident_ap, reads):
        tr.op('pe', lambda e: e.transpose(out_ap, in_ap, ident_ap), reads=reads, writes=[pskey])

    def LD(out, in_, writes, reads=(), q='sp', **kw):
        return tr.dma(q, out, in_, reads=reads, writes=writes, semkey=writes[0], **kw)

    def STORE(out, in_, reads, writes=(), q='sp', **kw):
        return tr.dma(q, out, in_, reads=reads, writes=writes, semkey=('st', reads[0]), **kw)

    ident = A.f32(128)
    identb = A.bf16(128)
    ones_f = A.f32(128)
    CT = [A.f32(128) for _ in range(4)]
    mods = A.f32(96)
    sc = A.f32(16)
    gs1 = A.f32(8); sh1 = A.f32(8); gs1c = A.f32(8); sh1c = A.f32(8); gs2 = A.f32(8); sh2 = A.f32(8)
    csh = A.f32(20); cs1 = A.f32(20)
    hb = A.f32(40)
    gvb = A.bf16(1024)
    gfb = A.f32(1024)
    wsT = A.bf16(8 * 128)
    bsf = A.f32(1024)
    hstate = A.f32(20)
    tiny = A.f32(64)
    zer = A.bf16(128)
    mh = A.f32(4)
    m25 = A.f32(2048)
    PH0 = A.mark()

    def cbc(kc): return CT[0][:, 56 + kc:57 + kc]
    def cwc(tap, kc): return CT[0][:, 16 + tap * 10 + kc:17 + tap * 10 + kc]
    def hbc(gi, d, kc): return hb[:, gi * 20 + d * 10 + kc:gi * 20 + d * 10 + kc + 1]
    def fcwc(tap, j):
        r = tap * 22 + j
        return CT[1][:, r:r + 1] if r < 128 else CT[2][:, r - 128:r - 127]
    def fcbc(j): return CT[2][:, 70 + j:71 + j]

    iot = A.f32(128)
    tr.op('pool', lambda e: e.iota(iot.bitcast(I32), [[1, 128]], base=0, channel_multiplier=-1), (), ['iot'])
    TS('dve', ident, iot.bitcast(I32), 0, None, ALU.is_equal, None, ['iot'], ['ident'])
    CP('dve', identb, ident, ['ident'], ['identb'])
    tr.op('dve', lambda e: e.memset(ones_f, 1.0), (), ['ones_f'])
    tr.op('dve', lambda e: e.memset(zer, 0.0), (), ['zer'])
    tr.op('dve', lambda e: e.memset(mh, -0.5), (), ['mh'])

    RT = [A.f32(128) for _ in range(4)]
    rows = [
        (0, 0, g1_d, 8), (0, 8, g2_d, 8), (0, 16, cw_d, 40), (0, 56, cb_d, 10), (0, 66, lam_d, 20),
        (0, 86, ba_d, 20), (0, 106, bx_d, 20),
        (1, 0, fcw_d[0:128, :], 128),
        (2, 0, fcw_d[128:198, :], 70), (2, 70, fcb_d, 22), (2, 92, c_d, 8), (2, 100, cctx_d, 8),
        (3, 0, bmod_d, 48),
    ]
    rtk = {i: [] for i in range(4)}
    for (ti, r0, src, n) in rows:
        rtk[ti].append(('RT', ti, r0))
    for i in range(4):
        tr.op('pool', lambda e, i=i: e.memset(RT[i], 0.0), (), rtk[i])
    for (ti, r0, src, n) in rows:
        tr.dma('sp', RT[ti][r0:r0 + n, :], src, reads=(), writes=[('RT', ti, r0)], semkey=('RT', ti, r0))
    for i in range(4):
        TR(PS[i][:, 0:128], psk(i), RT[i], ident, rtk[i] + ['ident'])
        CP('dve', CT[i], PS[i][:, 0:128], [psk(i)], [('CT', i)])

    sc3 = v3(sc, 2)
    ACT(sc3[:, :, 0], CT[2][:, 92:100], AF.Silu, [('CT', 2)], ['sc'])
    ACT(sc3[:, :, 1], CT[2][:, 100:108], AF.Silu, [('CT', 2), 'sc'], ['sc'])

    wm = [A.f32(8 * 1024) for _ in range(2)]
    mods3 = v3(mods, 2)
    for i in range(6):
        s = i % 2
        wm3 = v3(wm[s], 1024)
        LD(wm3, wmod_d[:, i * 1024:(i + 1) * 1024].rearrange("(kc p) n -> p kc n", p=128), [('wm', s)])
        for oc in range(8):
            col = (i * 8 + oc) * 2
            MM(PS[4][:, col:col + 2], psk(4),
               [(wm3[:, kc, oc * 128:(oc + 1) * 128], sc3[:, kc, :], [('wm', s), 'sc']) for kc in range(8)])
    psm3 = v3(PS[4][:, 0:96], 2)
    for n in range(2):
        TT('dve', mods3[:, :, n], psm3[:, :, n], CT[3][:, 0:48], ALU.add, [psk(4), ('CT', 3)], ['mods'])
    STT(gs1, mods3[:, 8:16, 0], 1.0, CT[0][:, 0:8], ALU.add, ALU.mult, ['mods', ('CT', 0)], ['gs1'])
    STT(gs1c, mods3[:, 8:16, 1], 1.0, CT[0][:, 0:8], ALU.add, ALU.mult, ['mods', ('CT', 0)], ['gs1c'])
    STT(gs2, mods3[:, 32:40, 0], 1.0, CT[0][:, 8:16], ALU.add, ALU.mult, ['mods', ('CT', 0)], ['gs2'])
    CP('dve', sh1, mods3[:, 0:8, 0], ['mods'], ['sh1'])
    CP('dve', sh1c, mods3[:, 0:8, 1], ['mods'], ['sh1c'])
    CP('dve', sh2, mods3[:, 24:32, 0], ['mods'], ['sh2'])
    dg = [A.f32(128) for _ in range(2)]
    for mi, mbase in enumerate((16, 40)):
        for kc in range(8):
            s = kc % 2
            TS('dve', dg[s], ident, mods3[:, mbase + kc, 0:1], None, ALU.mult, None, ['ident', 'mods'], [('dg', s)])
            b = 5 + (kc // 4)
            MM(PS[b][:, (kc % 4) * 128:(kc % 4 + 1) * 128], psk(b), [(ones_f, dg[s], ['ones_f', ('dg', s)])])
        for hf in range(2):
            TS('dve', m25[:, mi * 1024 + hf * 512: mi * 1024 + (hf + 1) * 512], PS[5 + hf], 0.5 if mi == 0 else 1.0, None,
               ALU.mult, None, [psk(5 + hf)], [('m25', mi)])
    lamc = CT[0][:, 66:86]
    ACT(tiny[:, 0:20], lamc, AF.Exp, [('CT', 0)], ['tiny0'], scale=-1.0)
    ACT(tiny[:, 20:40], tiny[:, 0:20], AF.Ln, ['tiny0'], ['tiny1'], bias=1.0)
    TS('dve', cs1, tiny[:, 20:40], -8.0, None, ALU.mult, None, ['tiny1'], ['cs'])
    TS('dve', csh, tiny[:, 20:40], -4.0, None, ALU.mult, None, ['tiny1'], ['cs'])
    TS('dve', hb, CT[0][:, 86:126], 0.5, None, ALU.mult, None, [('CT', 0)], ['hb'])
    stg = A.f32(1024)
    LD(stg, gv_d[0].partition_broadcast(128), ['stg'])
    CP('dve', gvb, stg, ['stg'], ['gvb'])
    LD(gfb, gf_d[0].partition_broadcast(128), ['gfb'])
    wsf = A.f32(8 * 128)
    wsf3 = v3(wsf, 128)
    LD(wsf3, ws_d.rearrange("(h p) q -> p h q", p=128), ['wsf'])
    for h in range(8):
        b = h // 4
        TR(PS[b][:, (h % 4) * 128:(h % 4 + 1) * 128], psk(b), wsf3[:, h, :], ident, ['wsf', 'ident'])
    for b in range(2):
        CP('dve', wsT[:, b * 512:(b + 1) * 512], PS[b], [psk(b)], ['wsT'])
    tr.dma('sp', v3(bsf[0:1, :], 128), bs_d.rearrange("p h -> h p").unsqueeze(0), reads=(), writes=['bsf'],
           semkey='bsf', allow_slow_non_contiguous=True)

    wst = [A.bf16(5632) for _ in range(2)]
    si = 0
    for kc in range(8):
        s = si % 2; si += 1
        parts = ((1280, 3328, 0), (3328, 3584, 2048), (4608, 6656, 2304))
        for pi, (c0, c1, o0) in enumerate(parts):
            tr.dma('pool', wst[s][:, o0:o0 + (c1 - c0)], win_d[kc * 128:(kc + 1) * 128, c0:c1], reads=(),
                   writes=[('wst', s, pi)], semkey=('wst', s, pi))
        STORE(d_win[:, :, kc * 128:(kc + 1) * 128].rearrange("j p c -> p j c"), v3(wst[s][:, 0:4352], 128),
              [('wst', s, pi) for pi in range(3)], ['d_win'])
    for kc in range(18):
        s = si % 2; si += 1
        src = wpr_d[kc * 128:(kc + 1) * 128, :] if kc < 10 else wpg_d[(kc - 10) * 128:(kc - 9) * 128, :]
        tr.dma('pool', wst[s][:, 0:1024], src, reads=(), writes=[('wst', s, pi) for pi in range(3)], semkey=('wst', s, 0))
        STORE(d_wpp[:, :, kc * 128:(kc + 1) * 128].rearrange("j p c -> p j c"), v3(wst[s][:, 0:1024], 128),
              [('wst', s, pi) for pi in range(3)], ['d_wpp'])
    for kc in range(8):
        s = si % 2; si += 1
        for pi, (c0, c1) in enumerate(((0, 2048), (2048, 4096), (4096, 5632))):
            tr.dma('pool', wst[s][:, c0:c1], wup_d[kc * 128:(kc + 1) * 128, c0:c1], reads=(),
                   writes=[('wst', s, pi)], semkey=('wst', s, pi))
        STORE(d_wup[:, :, kc * 128:(kc + 1) * 128].rearrange("j p c -> p j c"), v3(wst[s][:, 0:5632], 128),
              [('wst', s, pi) for pi in range(3)], ['d_wup'])
    STORE(d_pxr[:, :, 0:PADR].rearrange("j p t -> p j t"), v3(zer[:, 0:10 * PADR], PADR), ['zer'], ['d_pxr_pad'])
    STORE(d_pxr[:, :, PADR + T:PADR + T + PADR].rearrange("j p t -> p j t"), v3(zer[:, 0:10 * PADR], PADR), ['zer'], ['d_pxr_pad'])
    if debug and stop_after == 'P0':
        dsm = A.f32(1024)
        tr.op('dve', lambda e: e.memset(dsm, 0.0), (), ['dsm'])
        CP('dve', dsm[:, 0:128], CT[0], [('CT', 0), 'dsm'], ['dsm'])
        CP('dve', dsm[:, 128:224], mods, ['mods', 'dsm'], ['dsm'])
        CP('dve', dsm[:, 224:232], gs1, ['gs1', 'dsm'], ['dsm'])
        CP('dve', dsm[:, 232:240], sh1, ['sh1', 'dsm'], ['dsm'])
        CP('dve', dsm[:, 240:260], cs1, ['cs', 'dsm'], ['dsm'])
        CP('dve', dsm[:, 260:300], hb, ['hb', 'dsm'], ['dsm'])
        CP('dve', dsm[:, 300:428], m25[:, 0:128], [('m25', 0), 'dsm'], ['dsm'])
        CP('dve', dsm[:, 428:556], m25[:, 1024:1152], [('m25', 1), 'dsm'], ['dsm'])
        CP('dve', dsm[:, 556:684], wsT[:, 0:128], ['wsT', 'dsm'], ['dsm'])
        CP('dve', dsm[:, 684:812], bsf[:, 0:128], ['bsf', 'dsm'], ['dsm'])
        STORE(dbg_small, dsm, ['dsm'], ['dbg_small'])
    tr.barrier()
    A.reset(PH0)
    if stop_after == 'P0':
        return finish()

    def norm_to_fm(xt3, xtkey, ntb, gs_t, sh_t, hx3, hxkey, bufs, pbanks, tag, xnkey):
        junk, ssq, rstd, xnr = bufs
        for tb in range(ntb):
            ACT(junk, xt3[:, tb, :], AF.Square, [xtkey], [('junk', tag), ('ssq', tag)], accum_out=ssq[:, tb:tb + 1])
        TS('dve', rstd[:, 0:ntb], ssq[:, 0:ntb], 1.0 / D, EPS, ALU.mult, ALU.add, [('ssq', tag)], [('rstd', tag)])
        TT('pool', rstd[:, 0:ntb], rstd[:, 0:ntb], mh[:, 0:ntb], ALU.pow, [('rstd', tag), 'mh'], [('rstd', tag)])
        N = ntb * 128
        for tb in range(ntb):
            xs = tb % 2
            TS('pool', xnr[xs], xt3[:, tb, :], rstd[:, tb:tb + 1], 1.0, ALU.mult, ALU.mult,
               [xtkey, ('rstd', tag)], [(xnkey, xs)])
            for kc in range(8):
                b = pbanks[kc // 2]
                o = (kc % 2) * 512 + tb * 128
                TR(PSB[b][:, o:o + 128], psk(b), xnr[xs][:, kc * 128:(kc + 1) * 128], identb, [(xnkey, xs), 'identb'])
        for kc in range(8):
            b = pbanks[kc // 2]
            o = (kc % 2) * 512
            if (kc // 2) % 2 == 0:
                ACT(hx3[:, kc, 0:N], PSB[b][:, o:o + N], AF.Identity, [psk(b), gs_t[1], sh_t[1]], [(hxkey, kc)],
                    scale=gs_t[0][:, kc:kc + 1], bias=sh_t[0][:, kc:kc + 1])
            else:
                TS('dve', hx3[:, kc, 0:N], PSB[b][:, o:o + N], gs_t[0][:, kc:kc + 1], sh_t[0][:, kc:kc + 1], ALU.mult, ALU.add,
                   [psk(b), gs_t[1], sh_t[1]], [(hxkey, kc)])

    PXW = CH + 2 * PADR
    PXC = TCTX + 2 * PADR
    pxh = [A.bf16(10 * PXW) for _ in range(2)]
    pxc = A.bf16(10 * PXC)
    PH1 = A.mark()
    wrx = A.bf16(8 * WR)
    wrx3 = v3(wrx, WR)
    for kc in range(8):
        tr.dma('pool', wrx3[:, kc, :], win_d[kc * 128:(kc + 1) * 128, 0:WR], reads=(), writes=[('wrx', kc)], semkey=('wrx', kc))
    xt = [A.f32(4 * 1024) for _ in range(2)]
    junk = A.bf16(1024)
    ssq = [A.f32(4) for _ in range(2)]
    rstd = [A.f32(4) for _ in range(2)]
    xnr = [A.bf16(1024) for _ in range(2)]
    hx = [A.bf16(8 * CH) for _ in range(2)]
    pxo = [A.bf16(10 * CH) for _ in range(2)]

    def p1_chunk(k, s, src_rows, ntb, gs_t, sh_t, store=True):
        N = ntb * 128
        xt3 = v3(xt[s], 1024)
        LD(xt3[:, 0:ntb, :], src_rows.rearrange("(tb p) d -> p tb d", p=128), [('xt', s)])
        hx3 = v3(hx[s], CH)
        norm_to_fm(xt3, ('xt', s), ntb, gs_t, sh_t, hx3, ('hx', s), (junk, ssq[s], rstd[s], xnr), [0, 1, 2, 3], ('p1', s), 'xn1')
        hxk = [(('hx', s), kc) for kc in range(8)]
        po3 = v3(pxo[s], CH)
        for j in range(10):
            b = 4 + j % 4
            MM(PS[b][:, 0:N], psk(b), [(wrx3[:, kc, j * 128:(j + 1) * 128], hx3[:, kc, 0:N], [('wrx', kc), hxk[kc]]) for kc in range(8)])
            if j % 2 == 0:
                ACT(po3[:, j, 0:N], PS[b][:, 0:N], AF.Identity, [psk(b)], [('pxo', s, j)])
            else:
                CP('dve', po3[:, j, 0:N], PS[b][:, 0:N], [psk(b)], [('pxo', s, j)])
        if store:
            STORE(d_hxT[k], hx[s], hxk, [('d_hxT', k)])
            STORE(d_pxr[:, :, PADR + k * CH:PADR + (k + 1) * CH].rearrange("j p t -> p j t"), po3,
                  [('pxo', s, j) for j in range(10)], [('d_pxr', k)])

    p1_chunk(0, 0, ctx_d[:, :], 2, (gs1c, 'gs1c'), (sh1c, 'sh1c'), store=False)
    pxc3 = v3(pxc, PXC)
    tr.op('pool', lambda e: e.memset(pxc, 0.0), (), ['pxc'])
    CP('pool', pxc3[:, :, PADR:PADR + TCTX], v3(pxo[0], CH)[:, :, 0:TCTX], [('pxo', 0, j) for j in range(10)] + ['pxc'], ['pxc'])
    for k in range(NCH):
        p1_chunk(k, (k + 1) % 2, x_d[k * CH:(k + 1) * CH, :], 4, (gs1, 'gs1'), (sh1, 'sh1'))
    tr.barrier()
    A.reset(PH1)
    if stop_after == 'P1':
        return finish()

    dg1 = A.bf16(40 * 128)
    dg13 = v3(dg1, 128)
    for kc in range(10):
        for tap in range(4):
            TS('pool', dg13[:, kc * 4 + tap, :], ident, cwc(tap, kc), 1.0, ALU.mult, ALU.mult, ['ident', ('CT', 0)], ['dg1'])
    wg = A.bf16(2 * 5 * 2 * 256)
    wg5 = wg.rearrange("p (g h i j) -> p g h i j", g=2, h=5, i=2)

    def load_wg(d):
        for gi, wsrc in enumerate((wa_d, wx_d)):
            tr.dma('pool', wg5[:, gi], wsrc[d].rearrange("h (i p) j -> p h i j", p=128), reads=(), writes=[('wg', gi)], semkey=('wg', gi))

    xr_s = [A.bf16(10 * CH) for _ in range(2)]
    r_s = [A.bf16(5 * CH) for _ in range(2)]; i_s = [A.bf16(5 * CH) for _ in range(2)]
    a_s = [A.f32(5 * CH) for _ in range(2)]; e2_s = [A.f32(5 * CH) for _ in range(2)]
    m_s = [A.bf16(5 * CH) for _ in range(2)]
    lru_n = [0]
    yo = [A.bf16(10 * CH) for _ in range(2)]
    ybl = [A.bf16(10 * CH) for _ in range(2)]
    hst3 = v3(hstate, 10)

    def rev(ap3, kc, N):
        return bass.AP(ap3.tensor, ap3[:, kc, N - 1:N].offset, [list(ap3.ap[0]), [-1, N]])

    def lru_chunk(d, N, ph3, phkeys, y3, ykey, init_of, reverse):
        xs = lru_n[0] % 2
        lru_n[0] += 1
        xr3 = v3(xr_s[xs], CH)
        XR = lambda kc: ('xr', xs, kc)
        for kc in range(10):
            b = kc % 4
            MM(PS[b][:, 0:N], psk(b),
               [(dg13[:, kc * 4 + tap, :], ph3[:, kc, PADR + tap - 2:PADR + tap - 2 + N], ['dg1'] + phkeys) for tap in range(4)])
            if kc % 2 == 0:
                ACT(xr3[:, kc, 0:N], PS[b][:, 0:N], AF.Identity, [psk(b), ('CT', 0)], [XR(kc)], bias=cbc(kc))
            else:
                TS('dve', xr3[:, kc, 0:N], PS[b][:, 0:N], cbc(kc), None, ALU.add, None, [psk(b), ('CT', 0)], [XR(kc)])
        for hf in range(2):
            kcs = list(range(hf * 5, hf * 5 + 5))
            r3 = v3(r_s[hf], CH); i3 = v3(i_s[hf], CH); a3 = v3(a_s[hf], CH); e23 = v3(e2_s[hf], CH); m3 = v3(m_s[hf], CH)
            for kc in kcs:
                h, jc, q = kc // 2, kc % 2, kc - hf * 5
                for gi, dst3, nm in ((0, r3, 'r'), (1, i3, 'i')):
                    b = 4 + (kc * 2 + gi) % 4
                    MM(PS[b][:, 0:N], psk(b),
                       [(wg5[:, gi, h, ic, jc * 128:(jc + 1) * 128], xr3[:, 2 * h + ic, 0:N], [('wg', gi), XR(2 * h + ic)]) for ic in range(2)])
                    ACT(dst3[:, q, 0:N], PS[b][:, 0:N], AF.Tanh, [psk(b), 'hb'], [(nm, hf, q)], scale=0.5, bias=hbc(gi, d, kc))
            for kc in kcs:
                q = kc - hf * 5
                cc = d * 10 + kc
                ACT(a3[:, q, 0:N], r3[:, q, 0:N], AF.Exp, [('r', hf, q), 'cs'], [('a', hf, q)], scale=csh[:, cc:cc + 1], bias=csh[:, cc:cc + 1])
                ACT(e23[:, q, 0:N], r3[:, q, 0:N], AF.Exp, [('r', hf, q), 'cs'], [('e2', hf, q)], scale=cs1[:, cc:cc + 1], bias=cs1[:, cc:cc + 1])
            for kc in kcs:
                q = kc - hf * 5
                ACT(m3[:, q, 0:N], e23[:, q, 0:N], AF.Sqrt, [('e2', hf, q)], [('m', hf, q)], scale=-0.25, bias=0.25)
            for kc in kcs:
                q = kc - hf * 5
                STT(i3[:, q, 0:N], i3[:, q, 0:N], 1.0, xr3[:, kc, 0:N], ALU.add, ALU.mult, [('i', hf, q), XR(kc)], [('i', hf, q)])
                TT('dve', i3[:, q, 0:N], i3[:, q, 0:N], m3[:, q, 0:N], ALU.mult, [('i', hf, q), ('m', hf, q)], [('i', hf, q)])
                init_ap, init_keys = init_of(kc)
                if reverse:
                    o, aa, bb = rev(y3, kc, N), rev(a3, q, N), rev(i3, q, N)
                else:
                    o, aa, bb = y3[:, kc, 0:N], a3[:, q, 0:N], i3[:, q, 0:N]
                tr.op('dve', lambda e, o=o, aa=aa, bb=bb, init_ap=init_ap: e.tensor_tensor_scan(
                    out=o, data0=aa, data1=bb, initial=init_ap, op0=ALU.mult, op1=ALU.add),
                    reads=[('a', hf, q), ('i', hf, q)] + init_keys, writes=[(ykey, kc)])

    def ctx_pass(d):
        y3 = v3(yo[d], CH)
        lru_chunk(d, TCTX, pxc3, ['pxc'], y3, ('yo', d), lambda kc: (0.0, []), reverse=(d == 1))
        col = TCTX - 1 if d == 0 else 0
        CP('dve', hst3[:, d, :], y3[:, :, col], [(('yo', d), kc) for kc in range(10)], [('hst', d)])

    def lru_pass(d, order, post):
        prev = None

        def ld_px(n_it):
            k = order[n_it]
            s = n_it % 2
            LD(v3(pxh[s], PXW), d_pxr[:, :, k * CH:k * CH + PXW].rearrange("j p t -> p j t"), [('pxh', s)],
               reads=[('d_pxr', kk) for kk in (k - 1, k, k + 1) if 0 <= kk < NCH] + ['d_pxr_pad'])

        ld_px(0)
        for n_it, k in enumerate(order):
            s = n_it % 2
            ph3 = v3(pxh[s], PXW)
            if n_it + 1 < len(order):
                ld_px(n_it + 1)
            y3 = v3(yo[s], CH)
            if prev is None:
                init_of = lambda kc: (hst3[:, d, kc:kc + 1], [('hst', d)])
            else:
                py3 = v3(yo[prev], CH)
                pcol = 0 if d == 1 else CH - 1
                init_of = lambda kc, py3=py3, pcol=pcol, prev=prev: (py3[:, kc, pcol:pcol + 1], [(('yo', prev), kc)])
            lru_chunk(d, CH, ph3, [('pxh', s)], y3, ('yo', s), init_of, reverse=(d == 1))
            post(k, s, y3)
            prev = s

    def post_bwd(k, s, y3):
        STORE(d_yb[k], yo[s], [(('yo', s), kc) for kc in range(10)], [('d_yb', k)])

    def post_fwd(k, s, y3):
        LD(ybl[s], d_yb[k], [('ybl', s)], reads=[('d_yb', k)])
        yb3 = v3(ybl[s], CH)
        for kc in range(10):
            TT('pool', yb3[:, kc, :], y3[:, kc, :], yb3[:, kc, :], ALU.add, [(('yo', s), kc), ('ybl', s)], [('ybl', s)])
        STORE(d_yl[k], ybl[s], [('ybl', s)], [('d_yl', k)])

    load_wg(1)
    ctx_pass(1)
    if stop_after == 'LRUc':
        if debug:
            dsm = A.f32(1024)
            tr.op('dve', lambda e: e.memset(dsm, 0.0), (), ['dsm'])
            CP('dve', dsm[:, 0:20], hstate, [('hst', 1), 'dsm'], ['dsm'])
            CP('dve', dsm[:, 32:32 + 256], v3(yo[1], CH)[:, 3, 0:256], [(('yo', 1), 3), 'dsm'], ['dsm'])
            CP('dve', dsm[:, 320:320 + 256], xr3[:, 3, 0:256], [('xr', 3), 'dsm'], ['dsm'])
            STORE(dbg_small, dsm, ['dsm'], ['dbg_small'])
        tr.barrier()
        return finish()
    if stop_after == 'LRUb2':
        lru_pass(1, [NCH - 1, NCH - 2], post_bwd)
        tr.barrier()
        return finish()
    lru_pass(1, list(range(NCH - 1, -1, -1)), post_bwd)
    if stop_after == 'LRUb':
        tr.barrier()
        return finish()
    load_wg(0)
    ctx_pass(0)
    lru_pass(0, list(range(NCH)), post_fwd)
    tr.barrier()
    A.reset(PH0)
    if stop_after == 'LRU':
        return finish()

    wv = A.bf16(8 * 1024); wv3 = v3(wv, 1024)
    for kc in range(8):
        tr.dma('pool', wv3[:, kc, :], win_d[kc * 128:(kc + 1) * 128, 3584:4608], reads=(), writes=[('wv', kc)], semkey=('wv', kc))
    wvk = [('wv', kc) for kc in range(8)]
    wo = A.bf16(8 * 1024); wo3 = v3(wo, 1024)
    PHM = A.mark()
    wstg = A.f32(8 * 1024); wstg3 = v3(wstg, 1024)
    LD(wstg3, wout_d.rearrange("(kc p) n -> p kc n", p=128), ['wstg'])
    for kc in range(8):
        TT('dve', wo3[:, kc, :], wstg3[:, kc, :], m25[:, 0:1024], ALU.mult, ['wstg', ('m25', 0)], ['wo'])
    tr.barrier()
    A.reset(PHM)
    ring = [A.bf16(1024) for _ in range(8)]
    ppr = [A.bf16(18 * 128) for _ in range(2)]
    hxm = [A.bf16(8 * CH) for _ in range(2)]
    ylm = [A.bf16(10 * CH) for _ in range(2)]
    gtmp = [A.bf16(CH) for _ in range(2)]
    u_t = A.bf16(8 * CH)
    gvt = [A.bf16(1024) for _ in range(2)]
    vpp = A.bf16(4 * 1024)
    yg = A.bf16(8 * CH)
    gate = [A.bf16(CH) for _ in range(4)]
    t1 = [A.bf16(CH) for _ in range(2)]
    t2 = [A.bf16(CH) for _ in range(2)]
    mrg = A.bf16(8 * CH)
    xm = A.f32(4 * 1024)
    ssv = [A.f32(1) for _ in range(2)]
    ssq2 = A.f32(4); rstd2 = A.f32(4)
    xnr2 = [A.bf16(1024) for _ in range(2)]
    h2o = A.bf16(8 * CH)
    junk2 = A.bf16(1024)
    ringn = [0]

    def stream_slice(js):
        s = ringn[0] % 8
        ringn[0] += 1
        LD(ring[s], d_win[js], [('ring', s)], reads=['d_win'])
        return v3(ring[s], 128), ('ring', s)

    def m_loads(k):
        sb = k % 2
        LD(hxm[sb], d_hxT[k], [('hxm', sb)], reads=[('d_hxT', k)])
        LD(ylm[sb], d_yl[k], [('ylm', sb)], reads=[('d_yl', k)])

    def m_chunk(k):
        sb = k % 2
        HXK = ('hxm', sb); YLK = ('ylm', sb)
        hx3 = v3(hxm[sb], CH)
        xm3 = v3(xm, 1024)
        LD(xm3, x_d[k * CH:(k + 1) * CH, :].rearrange("(tb p) d -> p tb d", p=128), ['xm'])
        if k + 1 < NCH:
            m_loads(k + 1)
        yl3 = v3(ylm[sb], CH); u3 = v3(u_t, CH); yg3 = v3(yg, CH); mg3 = v3(mrg, CH)
        hk = [HXK]
        for j in range(10):
            w3, wk = stream_slice(j)
            b = j % 4
            MM(PS[b], psk(b), [(w3[:, kc, :], hx3[:, kc, :], [wk] + hk) for kc in range(8)])
            ACT(gtmp[j % 2], PS[b], AF.Gelu_apprx_tanh, [psk(b)], [('gtmp', j % 2)])
            TT('pool', yl3[:, j, :], yl3[:, j, :], gtmp[j % 2], ALU.mult, [YLK, ('gtmp', j % 2)], [YLK])
        for j in range(8):
            w3, wk = stream_slice(10 + j)
            b = (10 + j) % 4
            MM(PS[b], psk(b), [(w3[:, kc, :], hx3[:, kc, :], [wk] + hk) for kc in range(8)])
            ACT(u3[:, j, :], PS[b], AF.Gelu_apprx_tanh, [psk(b)], [('u', j)])
        vp3 = v3(vpp, 1024)
        for tb in range(4):
            gs_ = tb % 2
            for hf in range(2):
                b = 4 + (tb * 2 + hf) % 2
                MM(PS[b], psk(b), [(hx3[:, kc, tb * 128:(tb + 1) * 128], wv3[:, kc, hf * 512:(hf + 1) * 512], hk + [wvk[kc]]) for kc in range(8)])
                ACT(gvt[gs_][:, hf * 512:(hf + 1) * 512], PS[b], AF.Gelu_apprx_tanh, [psk(b)], [('gvt', gs_, hf)])
            gk = [('gvt', gs_, 0), ('gvt', gs_, 1)]
            ACT(junk2, gvt[gs_], AF.Square, gk, ['junk2v', ('ssv', gs_)], accum_out=ssv[gs_])
            TS('dve', ssv[gs_], ssv[gs_], 1.0 / 1024, EPS, ALU.mult, ALU.add, [('ssv', gs_)], [('ssv', gs_)])
            TT('pool', ssv[gs_], ssv[gs_], mh[:, 0:1], ALU.pow, [('ssv', gs_), 'mh'], [('ssv', gs_)])
            STT(vp3[:, tb, :], gvt[gs_], ssv[gs_], gvb, ALU.mult, ALU.mult, gk + [('ssv', gs_), 'gvb'], [('vpp', tb)])
        wsT3 = v3(wsT, 128); bsf3 = v3(bsf[0:1, :], 128)
        for h in range(8):
            b = 6 + h % 2
            for tb in range(4):
                MM(PS[b][:, tb * 128:(tb + 1) * 128], psk(b),
                   [(vp3[:, tb, h * 128:(h + 1) * 128], wsT3[:, h, :], [('vpp', tb), 'wsT']),
                    (ones_f[0:1, :], bsf3[0:1, h, :], ['ones_f', 'bsf'])])
            TT('dve', yg3[:, h, :], PS[b], u3[:, h, :], ALU.mult, [psk(b), ('u', h)], [('yg', h)])
        for j in range(8):
            sp_ = j % 2
            LD(ppr[sp_], d_wpp[j], [('ppr', sp_)], reads=['d_wpp'])
            pp3 = v3(ppr[sp_], 128)
            for gi in range(2):
                w3, wk = stream_slice(18 + gi * 8 + j)
                b = (j * 2 + gi) % 4
                MM(PS[b], psk(b), [(w3[:, kc, :], hx3[:, kc, :], [wk] + hk) for kc in range(8)])
                ACT(gate[sp_ * 2 + gi], PS[b], AF.Tanh, [psk(b)], [('gate', sp_ * 2 + gi)], scale=0.5)
            b1 = 4 + (j * 2) % 4
            MM(PS[b1], psk(b1), [(pp3[:, kc, :], yl3[:, kc, :], [('ppr', sp_), YLK]) for kc in range(10)])
            STT(t1[sp_], gate[sp_ * 2], 1.0, PS[b1], ALU.add, ALU.mult, [psk(b1), ('gate', sp_ * 2)], [('t1', sp_)])
            b2 = 4 + (j * 2 + 1) % 4
            MM(PS[b2], psk(b2), [(pp3[:, 10 + kc, :], yg3[:, kc, :], [('ppr', sp_), ('yg', kc)]) for kc in range(8)])
            STT(t2[sp_], gate[sp_ * 2 + 1], 1.0, PS[b2], ALU.add, ALU.mult, [psk(b2), ('gate', sp_ * 2 + 1)], [('t2', sp_)])
            TT('pool', mg3[:, j, :], t1[sp_], t2[sp_], ALU.add, [('t1', sp_), ('t2', sp_)], [('mrg', j)])
        mk = [('mrg', j) for j in range(8)]
        for tb in range(4):
            for hf in range(2):
                b = (tb * 2 + hf) % 4
                MM(PS[b], psk(b), [(mg3[:, kc, tb * 128:(tb + 1) * 128], wo3[:, kc, hf * 512:(hf + 1) * 512], [mk[kc], 'wo']) for kc in range(8)])
                TT('dve', xm3[:, tb, hf * 512:(hf + 1) * 512], PS[b], xm3[:, tb, hf * 512:(hf + 1) * 512], ALU.add, [psk(b), 'xm'], ['xm'])
        STORE(d_x1[k * CH:(k + 1) * CH, :].rearrange("(tb p) d -> p tb d", p=128), xm3, ['xm'], [('d_x1', k)])
        h23 = v3(h2o, CH)
        norm_to_fm(xm3, 'xm', 4, (gs2, 'gs2'), (sh2, 'sh2'), h23, 'h2o', (junk2, ssq2, rstd2, xnr2), [4, 5, 6, 7], 'm', 'xn2')
        STORE(d_h2T[:, :, PADH + k * CH:PADH + (k + 1) * CH].rearrange("kc p t -> p kc t"), h23,
              [('h2o', kc) for kc in range(8)], [('d_h2T', k)])

    m_loads(0)
    for k in range(NCH):
        m_chunk(k)
    tr.barrier()
    A.reset(PH0)
    if stop_after == 'M':
        return finish()

    wd = A.bf16(NJ * 1024); wd3 = v3(wd, 1024)
    PHF = A.mark()
    wstg = A.f32(8 * 1024); wstg3 = v3(wstg, 1024)
    for g0 in range(0, NJ, 8):
        gn = min(8, NJ - g0)
        LD(wstg3[:, 0:gn, :], wdn_d[g0 * 128:(g0 + gn) * 128, :].rearrange("(kc p) n -> p kc n", p=128), ['wstg'])
        for kc in range(gn):
            TT('dve', wd3[:, g0 + kc, :], wstg3[:, kc, :], m25[:, 1024:2048], ALU.mult, ['wstg', ('m25', 1)], ['wd'])
    tr.barrier()
    A.reset(PHF)
    dg2 = A.bf16(NJ * 9 * 128)
    dg2v = v3(dg2, 128)
    for j in range(NJ):
        for tap in range(9):
            TS('pool', dg2v[:, j * 9 + tap, :], ident, fcwc(tap, j), 1.0, ALU.mult, ALU.mult, ['ident', ('CT', 1), ('CT', 2)], ['dg2'])
    ringf = [A.bf16(1024) for _ in range(4)]
    HW_ = CH + 2 * PADH
    h2h = [A.bf16(8 * HW_) for _ in range(2)]
    gbuf = [A.bf16(HW_) for _ in range(2)]
    ggl = [A.bf16(CH) for _ in range(2)]
    actb = A.bf16(NJ * CH)
    x1t = A.f32(4 * 1024)
    ssq3 = A.f32(4); rstd3 = A.f32(4)
    junk3 = A.bf16(1024)
    outt = [A.f32(1024) for _ in range(2)]
    ringfn = [0]

    def f_loads(k):
        sb = k % 2
        h3 = v3(h2h[sb], HW_)
        lo = k * CH
        c0 = PADH if k == 0 else 0
        c1 = HW_ - PADH if k == NCH - 1 else HW_
        LD(h3[:, :, c0:c1], d_h2T[:, :, lo + c0:lo + c1].rearrange("kc p t -> p kc t"), [('h2h', sb)],
           reads=[('d_h2T', kk) for kk in (k - 1, k, k + 1) if 0 <= kk < NCH])

    def f_chunk(k):
        sb = k % 2
        H2K = ('h2h', sb)
        h3 = v3(h2h[sb], HW_)
        x13 = v3(x1t, 1024)
        LD(x13, d_x1[k * CH:(k + 1) * CH, :].rearrange("(tb p) d -> p tb d", p=128), ['x1t'], reads=[('d_x1', k)])
        if k + 1 < NCH:
            f_loads(k + 1)
        ab3 = v3(actb, CH)
        for j in range(NJ):
            sg = j % 2
            d3 = dg2v[:, j * 9:(j + 1) * 9, :]
            sl = []
            for part in range(2):
                s = ringfn[0] % 4
                ringfn[0] += 1
                LD(ringf[s], d_wup[part * NJ + j], [('ringf', s)], reads=['d_wup'])
                sl.append((v3(ringf[s], 128), ('ringf', s)))
            gb = gbuf[sg]
            w3, wk = sl[0]
            bm = (j % 2) * 4
            MM(PS[bm], psk(bm), [(w3[:, kc, :], h3[:, kc, PADH:PADH + CH], [wk, H2K]) for kc in range(8)])
            ACT(gb[:, PADH:PADH + CH], PS[bm], AF.Identity, [psk(bm)], [('gb', sg, 1)])
            bh = bm + 1
            if k > 0:
                MM(PS[bh][:, 0:PADH], psk(bh), [(w3[:, kc, :], h3[:, kc, 0:PADH], [wk, H2K]) for kc in range(8)])
            if k < NCH - 1:
                MM(PS[bh][:, PADH:2 * PADH], psk(bh), [(w3[:, kc, :], h3[:, kc, PADH + CH:HW_], [wk, H2K]) for kc in range(8)])
            if k > 0:
                CP('dve', gb[:, 0:PADH], PS[bh][:, 0:PADH], [psk(bh)], [('gb', sg, 0)])
            else:
                tr.op('dve', lambda e, gb=gb: e.memset(gb[:, 0:PADH], 0.0), [], [('gb', sg, 0)])
            if k < NCH - 1:
                CP('dve', gb[:, PADH + CH:HW_], PS[bh][:, PADH:2 * PADH], [psk(bh)], [('gb', sg, 2)])
            else:
                tr.op('dve', lambda e, gb=gb: e.memset(gb[:, PADH + CH:HW_], 0.0), [], [('gb', sg, 2)])
            w3v, wkv = sl[1]
            bv = bm + 3
            MM(PS[bv], psk(bv), [(w3v[:, kc, :], h3[:, kc, PADH:PADH + CH], [wkv, H2K]) for kc in range(8)])
            gbk = [('gb', sg, i_) for i_ in range(3)]
            bc = bm + 2
            ops = []
            for dr in (0, -1, 1):
                tap = (dr + 1) * 3 + 1
                ops.append((d3[:, tap, :], gb[:, PADH + dr * 64:PADH + dr * 64 + CH], PS[bc]))
            pso = v3(PS[bc], 64)
            for dr in (0, -1, 1):
                src = v3(gb[:, PADH + dr * 64:PADH + dr * 64 + CH], 64)
                ops.append((d3[:, (dr + 1) * 3 + 0, :], src[:, :, 0:63], pso[:, :, 1:64]))
                ops.append((d3[:, (dr + 1) * 3 + 2, :], src[:, :, 1:64], pso[:, :, 0:63]))
            for i_, (l, r, o) in enumerate(ops):
                tr.op('pe', lambda e, l=l, r=r, o=o, i_=i_: e.matmul(o, l, r, start=(i_ == 0), stop=(i_ == 8)),
                      reads=['dg2'] + gbk, writes=[psk(bc)])
            ACT(ggl[sg], PS[bc], AF.Gelu_apprx_tanh, [psk(bc), ('CT', 2)], [('ggl', sg)], bias=fcbc(j))
            TT('dve', ab3[:, j, :], PS[bv], ggl[sg], ALU.mult, [psk(bv), ('ggl', sg)], [('actb', j)])
        ak = [('actb', j) for j in range(NJ)]
        for tb in range(4):
            for hf in range(2):
                b = (tb * 2 + hf) % 4
                MM(PS[b], psk(b), [(ab3[:, j, tb * 128:(tb + 1) * 128], wd3[:, j, hf * 512:(hf + 1) * 512], [ak[j], 'wd']) for j in range(NJ)])
                TT('dve', x13[:, tb, hf * 512:(hf + 1) * 512], PS[b], x13[:, tb, hf * 512:(hf + 1) * 512], ALU.add, [psk(b), 'x1t'], ['x1t'])
            ACT(junk3, x13[:, tb, :], AF.Square, ['x1t'], ['junk3', ('ssq3', tb)], accum_out=ssq3[:, tb:tb + 1])
            TS('dve', rstd3[:, tb:tb + 1], ssq3[:, tb:tb + 1], 1.0 / D, EPS, ALU.mult, ALU.add, [('ssq3', tb)], [('rstd3', tb)])
            TT('pool', rstd3[:, tb:tb + 1], rstd3[:, tb:tb + 1], mh[:, 0:1], ALU.pow, [('rstd3', tb), 'mh'], [('rstd3', tb)])
            so = tb % 2
            STT(outt[so], x13[:, tb, :], rstd3[:, tb:tb + 1], gfb, ALU.mult, ALU.mult, ['x1t', ('rstd3', tb), 'gfb'], [('outt', so)])
            STORE(out_d[k * CH + tb * 128:k * CH + (tb + 1) * 128, :], outt[so], [('outt', so)], [('out', k, tb)])

    f_loads(0)
    for k in range(NCH):
        f_chunk(k)
    return finish()


_INPUT_ORDER = ["x", "c", "ctx", "c_ctx", "w_mod", "b_mod", "g_norm1", "g_norm2", "w_in", "conv_w", "conv_b",
                "lru_lam", "lru_ba", "lru_bx", "lru_wa", "lru_wx", "g_v", "w_s", "b_s", "w_pr", "w_pg", "w_out",
                "w_up", "ffn_conv_w", "ffn_conv_b", "w_down", "g_final"]


def make_in_maps(inputs, cores):
    f = lambda a: np.ascontiguousarray(np.asarray(a, dtype=np.float32))
    sh = {
        "w_mod": f(inputs["w_mod"][0]), "b_mod": f(inputs["b_mod"][0]).reshape(48, 128),
        "g_norm1": f(inputs["g_norm1"][0]).reshape(8, 128), "g_norm2": f(inputs["g_norm2"][0]).reshape(8, 128),
        "w_in": f(inputs["w_in"][0]), "conv_w": f(inputs["conv_w"][0]).reshape(40, 128),
        "conv_b": f(inputs["conv_b"][0]).reshape(10, 128), "lru_lam": f(inputs["lru_lam"][0]).reshape(20, 128),
        "lru_ba": f(inputs["lru_ba"][0]).reshape(20, 128), "lru_bx": f(inputs["lru_bx"][0]).reshape(20, 128),
        "lru_wa": f(inputs["lru_wa"][0]), "lru_wx": f(inputs["lru_wx"][0]),
        "g_v": f(inputs["g_v"][0]).reshape(1, D), "w_s": f(inputs["w_s"][0]).reshape(8 * 128, 128),
        "b_s": f(inputs["b_s"][0]), "w_pr": f(inputs["w_pr"][0]), "w_pg": f(inputs["w_pg"][0]),
        "w_out": f(inputs["w_out"][0]), "w_up": f(inputs["w_up"][0]),
        "ffn_conv_w": f(inputs["ffn_conv_w"][0]).reshape(198, 128), "ffn_conv_b": f(inputs["ffn_conv_b"][0]).reshape(22, 128),
        "w_down": f(inputs["w_down"][0]), "g_final": f(inputs["g_final"]).reshape(1, D),
        "c_ctx": f(inputs["c_ctx"]).reshape(8, 128),
    }
    maps = []
    for b in cores:
        m = dict(sh)
        m["x"] = f(inputs["x"][b]); m["c"] = f(inputs["c"][b]).reshape(8, 128); m["ctx"] = f(inputs["ctx"][b])
        maps.append(m)
    return maps


def kernel(**inputs):
    nc = build_program(debug=False)
    maps = make_in_maps(inputs, list(range(8)))
    res = run_bass_kernel_spmd(nc, maps, core_ids=list(range(8)))
    out = np.stack([np.asarray(r["out"], dtype=np.float32).reshape(T, D) for r in res.results], axis=0)
    return out
```
